# Optimizing a Trainium2 kernel written in Bass

```python
import math
import jax, jax.numpy as jnp
from jax import lax
import numpy as np

D_MODEL = 2048
BATCH = 2
SEQ = 16384
DEPTH = 1

D_MIX = D_MODEL
GM_GROUPS = 8
GM_DIM = 128
GM_WIDTH = GM_GROUPS * GM_DIM
CHUNK = 128
NA_HEADS = 16
HEAD_DIM = 64
NA_WIDTH = NA_HEADS * HEAD_DIM
IN_WIDTH = 2 * GM_WIDTH + 3 * NA_WIDTH
GRID_W = 64
NA_ROWS = 8
NA_COLS = 16
N_KEYS = 128
N_EXPERTS = N_KEYS * N_KEYS
PEER_HEADS = 8
PEER_QDIM = 256
PEER_QHALF = PEER_QDIM // 2
PEER_TOPK = 16
PEER_CHUNK = 128
ALPHA = (2.0 * DEPTH) ** 0.25
BETA = (8.0 * DEPTH) ** -0.25
LN_EPS = 1e-5

kernel_name = "hymba_gmlp_natten_peer_deepnorm"


def _layernorm(x, g, b):
    xf = x.astype(jnp.float32)
    mu = jnp.mean(xf, axis=-1, keepdims=True)
    var = jnp.mean(jnp.square(xf - mu), axis=-1, keepdims=True)
    y = (xf - mu) * lax.rsqrt(var + LN_EPS)
    return (y * g.astype(jnp.float32) + b.astype(jnp.float32)).astype(x.dtype)


def _spatial_gating(u, v, v_g, v_b, w_s, b_s):
    B, S, _ = u.shape
    v = _layernorm(v, v_g, v_b)
    v = v.reshape(B, S // CHUNK, CHUNK, GM_GROUPS, GM_DIM)
    sv = jnp.einsum('gpq,bnqgc->bnpgc', w_s, v) + b_s.T[None, None, :, :, None]
    return u * sv.reshape(B, S, GM_WIDTH)


def _neighbourhood_attention(q, k, v, rpb):
    B, S = q.shape[0], q.shape[1]
    rows = S // GRID_W
    kr = min(NA_ROWS, rows)

    def to_grid(t):
        return t.reshape(B, rows, GRID_W, NA_HEADS, HEAD_DIM).transpose(0, 3, 1, 2, 4)

    qg, kg, vg = to_grid(q), to_grid(k), to_grid(v)
    col_start = np.clip(np.arange(GRID_W) - NA_COLS // 2, 0, GRID_W - NA_COLS)
    col_idx = col_start[:, None] + np.arange(NA_COLS)[None, :]
    col_off = col_idx - np.arange(GRID_W)[:, None] + (NA_COLS - 1)
    scale = HEAD_DIM ** -0.5

    def row_block(r):
        rs = jnp.clip(r - kr // 2, 0, rows - kr)
        q_r = lax.dynamic_index_in_dim(qg, r, axis=2, keepdims=False)
        k_w = lax.dynamic_slice_in_dim(kg, rs, kr, axis=2)[:, :, :, col_idx]
        v_w = lax.dynamic_slice_in_dim(vg, rs, kr, axis=2)[:, :, :, col_idx]
        row_off = rs + jnp.arange(kr) - r + (NA_ROWS - 1)
        bias = jnp.take(rpb, row_off, axis=1)[:, :, col_off]
        s = (jnp.einsum('bhcd,bhicjd->bhcij', q_r, k_w).astype(jnp.float32) * scale
             + bias.transpose(0, 2, 1, 3)[None].astype(jnp.float32))
        p = jax.nn.softmax(s.reshape(B, NA_HEADS, GRID_W, kr * NA_COLS), axis=-1)
        p = p.reshape(s.shape).astype(v.dtype)
        return jnp.einsum('bhcij,bhicjd->bhcd', p, v_w)

    out = lax.map(row_block, jnp.arange(rows))
    return out.transpose(1, 0, 3, 2, 4).reshape(B, S, NA_WIDTH)


def _peer(x, wq, subkeys, u_tab, v_tab):
    B, S, D = x.shape
    T = B * S
    xf = x.reshape(T, D)
    q = (xf @ wq).reshape(T, PEER_HEADS, 2, PEER_QHALF).astype(jnp.float32)
    s = jnp.einsum('thpk,hpnk->thpn', q, subkeys.astype(jnp.float32))
    top_s, top_i = lax.top_k(s, PEER_TOPK)
    cand_s = top_s[:, :, 0, :, None] + top_s[:, :, 1, None, :]
    cand_i = top_i[:, :, 0, :, None] * N_KEYS + top_i[:, :, 1, None, :]
    best_s, pos = lax.top_k(cand_s.reshape(T, PEER_HEADS, PEER_TOPK * PEER_TOPK), PEER_TOPK)
    idx = jnp.take_along_axis(cand_i.reshape(T, PEER_HEADS, PEER_TOPK * PEER_TOPK), pos, axis=-1)
    gate = jax.nn.softmax(best_s, axis=-1).astype(x.dtype)
    nb = T // PEER_CHUNK
    idx = idx.reshape(nb, PEER_CHUNK, PEER_HEADS * PEER_TOPK)
    gate = gate.reshape(nb, PEER_CHUNK, PEER_HEADS * PEER_TOPK)
    xc = xf.reshape(nb, PEER_CHUNK, D)

    def apply_experts(args):
        xb, ib, gb = args
        h = jnp.einsum('td,tkd->tk', xb, u_tab[ib])
        w = gb * jax.nn.gelu(h, approximate=False)
        return jnp.einsum('tk,tkd->td', w, v_tab[ib])

    y = lax.map(apply_experts, (xc, idx, gate))
    return y.reshape(B, S, D)


def setup_inputs(seed: int = 0) -> dict:
    key = jax.random.key(seed)
    ks = jax.random.split(key, 24)
    f32 = jnp.float32
    nrm = lambda k, shape, s: jax.random.normal(k, shape, f32) * s
    L = DEPTH
    x = jax.random.normal(ks[0], (BATCH, SEQ, D_MODEL), f32)
    w_in_gm_qk = nrm(ks[1], (L, D_MODEL, 2 * GM_WIDTH + 2 * NA_WIDTH), D_MODEL ** -0.5)
    w_in_val = nrm(ks[2], (L, D_MODEL, NA_WIDTH), BETA * D_MODEL ** -0.5)
    w_in = jnp.concatenate([w_in_gm_qk, w_in_val], axis=-1)
    b_in = nrm(ks[3], (L, IN_WIDTH), 0.01)
    v_norm_g = 1.0 + nrm(ks[4], (L, GM_WIDTH), 0.02)
    v_norm_b = nrm(ks[5], (L, GM_WIDTH), 0.01)
    w_spatial = nrm(ks[6], (L, GM_GROUPS, CHUNK, CHUNK), CHUNK ** -0.5)
    b_spatial = 1.0 + nrm(ks[7], (L, GM_GROUPS, CHUNK), 0.01)
    rpb = nrm(ks[8], (L, NA_HEADS, 2 * NA_ROWS - 1, 2 * NA_COLS - 1), 0.02)
    w_out = nrm(ks[9], (L, D_MIX, D_MODEL), BETA * D_MIX ** -0.5)
    b_out = nrm(ks[10], (L, D_MODEL), 0.01)
    ln1_g = 1.0 + nrm(ks[11], (L, D_MODEL), 0.02)
    ln1_b = nrm(ks[12], (L, D_MODEL), 0.01)
    peer_wq = nrm(ks[13], (L, D_MODEL, PEER_HEADS * PEER_QDIM), D_MODEL ** -0.5)
    peer_subkeys = nrm(ks[14], (L, PEER_HEADS, 2, N_KEYS, PEER_QHALF), PEER_QHALF ** -0.5)
    peer_u = nrm(ks[15], (L, N_EXPERTS, D_MODEL), D_MODEL ** -0.5)
    peer_v = nrm(ks[16], (L, N_EXPERTS, D_MODEL), BETA * PEER_HEADS ** -0.5)
    ln2_g = 1.0 + nrm(ks[17], (L, D_MODEL), 0.02)
    ln2_b = nrm(ks[18], (L, D_MODEL), 0.01)
    return {"x": x, "w_in": w_in, "b_in": b_in, "v_norm_g": v_norm_g, "v_norm_b": v_norm_b,
            "w_spatial": w_spatial, "b_spatial": b_spatial, "rpb": rpb,
            "w_out": w_out, "b_out": b_out, "ln1_g": ln1_g, "ln1_b": ln1_b,
            "peer_wq": peer_wq, "peer_subkeys": peer_subkeys, "peer_u": peer_u, "peer_v": peer_v,
            "ln2_g": ln2_g, "ln2_b": ln2_b}


def reference(x, w_in, b_in, v_norm_g, v_norm_b, w_spatial, b_spatial, rpb,
              w_out, b_out, ln1_g, ln1_b, peer_wq, peer_subkeys, peer_u, peer_v,
              ln2_g, ln2_b):
    B, S, _ = x.shape
    for l in range(DEPTH):
        h = x @ w_in[l] + b_in[l]
        gm = jax.nn.gelu(h[..., :2 * GM_WIDTH], approximate=False)
        u, v = gm[..., :GM_WIDTH], gm[..., GM_WIDTH:]
        a_out = _spatial_gating(u, v, v_norm_g[l], v_norm_b[l], w_spatial[l], b_spatial[l])
        qkv = h[..., 2 * GM_WIDTH:].reshape(B, S, 3, NA_HEADS, HEAD_DIM)
        b_out_na = _neighbourhood_attention(qkv[:, :, 0], qkv[:, :, 1], qkv[:, :, 2], rpb[l])
        mix = jnp.concatenate([a_out, b_out_na], axis=-1) @ w_out[l] + b_out[l]
        x = _layernorm(ALPHA * x + mix, ln1_g[l], ln1_b[l])
        ffn = _peer(x, peer_wq[l], peer_subkeys[l], peer_u[l], peer_v[l])
        x = _layernorm(ALPHA * x + ffn, ln2_g[l], ln2_b[l])
    return x
```

```python
import numpy as np
from contextlib import ExitStack
import concourse.bass as bass
import concourse.mybir as mybir
from concourse.bass_utils import run_bass_kernel_spmd

F32, BF16 = mybir.dt.float32, mybir.dt.bfloat16
AF = mybir.ActivationFunctionType
ALU = mybir.AluOpType
AX = mybir.AxisListType

ALPHA = 2.0 ** 0.25
LN_EPS = 1e-5
NEG = -30000.0
TAU_EPS = 2e-4
ENGS = ("pe", "act", "dve", "pool", "sp")
EPOCH = 30000


class Buf:
    def __init__(self, name, t):
        self.name = name
        self.t = t
        self.last_w = None
        self.reads = []
        self.dsem = None
        self.dcount = 0

    def __getitem__(self, k):
        return self.t[k]


class Sched:
    def __init__(self, nc, stack):
        self.nc = nc
        self.stack = stack
        self.ops = {e: [] for e in ENGS}
        self.ecount = {e: 0 for e in ENGS}
        self.dbufs = []

    def _dsem(self, b):
        if b.dsem is None:
            b.dsem = self.stack.enter_context(self.nc.semaphore("d_" + b.name))
            self.dbufs.append(b)
        return b.dsem

    def _deps(self, reads, writes, ww_eng=None):
        deps = []
        for b in reads:
            if b.last_w is not None:
                deps.append(b.last_w)
        for b in writes:
            if b.last_w is not None:
                lw = b.last_w
                if not (ww_eng is not None and lw[0] == 'e' and lw[1] == ww_eng):
                    deps.append(lw)
            deps.extend(b.reads)
        return deps

    def op(self, eng, fn, reads=(), writes=(), ww_ok=False):
        deps = self._deps(reads, writes, eng if ww_ok else None)
        idx = self.ecount[eng]
        self.ecount[eng] += 1
        me = ('e', eng, idx)
        for b in reads:
            b.reads.append(me)
        for b in writes:
            b.last_w = me
            b.reads = []
        self.ops[eng].append([deps, fn, ('e', idx)])

    def dma(self, eng, fn, rd=None, wr=None, extra=()):
        b = rd if rd is not None else wr
        deps = self._deps([rd] if rd is not None else [], [wr] if wr is not None else [])
        for x in extra:
            if x.last_w is not None:
                deps.append(x.last_w)
        sem = self._dsem(b)
        b.dcount += 16
        me = ('d', b, b.dcount)
        if rd is not None:
            b.reads.append(me)
        else:
            b.last_w = me
            b.reads = []
        self.ops[eng].append([deps, fn, ('d', sem)])

    def barrier(self):
        deps = []
        for e in ENGS:
            if self.ecount[e] > 0:
                deps.append(('e', e, self.ecount[e] - 1))
        for b in self.dbufs:
            deps.append(('d', b, b.dcount))
        for e in ENGS:
            self.ops[e].append([list(deps), None, None])

    def finalize(self):
        signal = {e: set() for e in ENGS}
        for e in ENGS:
            view = {}
            for op in self.ops[e]:
                waits = []
                for d in op[0]:
                    if d[0] == 'e':
                        _, e2, idx = d
                        if e2 == e and e == "pe":
                            continue
                        if e2 == e and op[1] is None:
                            continue
                        key = ('e', e2)
                        if view.get(key, -1) >= idx:
                            continue
                        view[key] = idx
                        signal[e2].add(idx)
                        waits.append(d)
                    else:
                        _, b, cnt = d
                        key = ('d', b.name)
                        if view.get(key, 0) >= cnt:
                            continue
                        view[key] = cnt
                        waits.append(d)
                op[0] = waits
        self.rank = {}
        self.esems = {}
        for e in ENGS:
            srt = sorted(signal[e])
            self.rank[e] = {idx: r for r, idx in enumerate(srt)}
            nep = (len(srt) + EPOCH - 1) // EPOCH
            self.esems[e] = [self.stack.enter_context(self.nc.semaphore("s_%s%d" % (e, k)))
                             for k in range(max(nep, 1))]


    def check(self):
        pos = {e: 0 for e in ENGS}
        done_e = {e: -1 for e in ENGS}
        dcnt = {}
        total = sum(len(self.ops[e]) for e in ENGS)
        ndone = 0
        while ndone < total:
            progressed = False
            for e in ENGS:
                while pos[e] < len(self.ops[e]):
                    waits, fn, tag = self.ops[e][pos[e]]
                    ok = True
                    for d in waits:
                        if d[0] == 'e':
                            if done_e[d[1]] < d[2]:
                                ok = False
                                break
                        else:
                            if dcnt.get(d[1].name, 0) < d[2]:
                                ok = False
                                break
                    if not ok:
                        break
                    if tag is not None:
                        if tag[0] == 'e':
                            done_e[e] = tag[1]
                        else:
                            nm = [b for b in self.dbufs if b.dsem is tag[1]][0].name
                            dcnt[nm] = dcnt.get(nm, 0) + 16
                    pos[e] += 1
                    ndone += 1
                    progressed = True
            if not progressed:
                msg = []
                for e in ENGS:
                    if pos[e] < len(self.ops[e]):
                        waits = self.ops[e][pos[e]][0]
                        msg.append((e, pos[e], len(self.ops[e]), [(d[0], d[1] if d[0] == 'e' else d[1].name, d[2]) for d in waits], dict(done_e)))
                raise RuntimeError("DEADLOCK: %r" % (msg,))
        return True

    def emit(self, block):
        sched = self

        def run(eng_name):
            def body(e):
                for waits, fn, tag in sched.ops[eng_name]:
                    for d in waits:
                        if d[0] == 'e':
                            r = sched.rank[d[1]][d[2]]
                            e.wait_ge(sched.esems[d[1]][r // EPOCH], r % EPOCH + 1)
                        else:
                            e.wait_ge(d[1].dsem, d[2])
                    if fn is None:
                        continue
                    ins = fn(e)
                    if tag[0] == 'd':
                        ins.then_inc(tag[1], 16)
                    else:
                        r = sched.rank[eng_name].get(tag[1])
                        if r is not None:
                            ins.then_inc(sched.esems[eng_name][r // EPOCH], 1)
            return body

        block.tensor(run("pe"))
        block.scalar(run("act"))
        block.vector(run("dve"))
        block.gpsimd(run("pool"))
        block.sync(run("sp"))


class Arena:
    def __init__(self, t, n):
        self.t = t
        self.n = n
        self.off = 0

    def alloc(self, name, shape, dt=F32):
        p = shape[0]
        n = int(np.prod(shape[1:]))
        if dt == F32:
            ap = self.t[0:p, self.off:self.off + n]
            self.off += n
        else:
            nf = (n + 1) // 2
            ap = self.t[0:p, self.off:self.off + nf].bitcast(BF16)[:, 0:n]
            self.off += nf
        self.off = (self.off + 7) // 8 * 8
        assert self.off <= self.n, ("arena overflow", name, self.off, self.n)
        if len(shape) == 3:
            ap = ap.rearrange("p (a b) -> p a b", a=shape[1])
        elif len(shape) == 4:
            ap = ap.rearrange("p (a b c) -> p a b c", a=shape[1], b=shape[2])
        return Buf(name, ap)


def build(debug=None):
    nc = bass.Bass("TRN2", target_bir_lowering=False)

    def din(name, shape):
        return nc.dram_tensor(name, list(shape), F32, kind="ExternalInput").ap()

    def dscr(name, shape, dt=BF16):
        return nc.dram_tensor(name, list(shape), dt, kind="Internal").ap()

    xT = din("xT", [2048, 4608])
    xtok = din("xtok", [4096, 2048])
    w_in = din("w_in", [2048, 5120])
    w_out = din("w_out", [2048, 2048])
    wq = din("wq", [2048, 2048])
    uT = din("uT", [2048, 16384])
    vtab = din("vtab", [16384, 2048])
    bcol_d = din("bcol", [128, 40])
    brow_in_d = din("brow_in", [1, 2048])
    brow_out_d = din("brow_out", [1, 2048])
    bs_row_d = din("bs_row", [1, 1024])
    vng_d = din("vng", [128, 1024])
    vnb_d = din("vnb", [128, 1024])
    ln1g_d = din("ln1g", [128, 2048])
    ln1b_d = din("ln1b", [128, 2048])
    ln2g_d = din("ln2g", [128, 2048])
    ln2b_d = din("ln2b", [128, 2048])
    wsT_d = din("wsT", [128, 1024])
    skT_d = din("skT", [128, 2048])
    bias_d = din("biasT", [5, 128, 10240])
    ident_d = din("ident", [128, 128])
    outp = nc.dram_tensor("out", [4096, 2048], F32, kind="ExternalOutput").ap()

    w_in_bf = dscr("w_in_bf", [2048, 5120])
    w_out_bf = dscr("w_out_bf", [2048, 2048])
    wq_bf = dscr("wq_bf", [2048, 2048])
    uT_bf = dscr("uT_bf", [2048, 16384])
    v_bf = dscr("v_bf", [16384, 2048])
    xT_bf = dscr("xT_bf", [2048, 4608])
    bias_bf = dscr("bias_bf", [5, 128, 10240])
    wsT_bf = dscr("wsT_bf", [128, 1024])
    skT_bf = dscr("skT_bf", [128, 2048])
    ident_bfd = dscr("ident_bfd", [128, 128])
    qT_s = dscr("qT_s", [1024, 4608])
    kT_s = dscr("kT_s", [1024, 4608])
    vaug_s = dscr("vaug_s", [4608, 1040])
    AT_s = dscr("AT_s", [1024, 4096])
    xn_s = dscr("xn_s", [4096, 2048], F32)
    xnT_s = dscr("xnT_s", [2048, 4096])

    stack = ExitStack()
    with stack:
        ARN = 52400
        at = stack.enter_context(nc.sbuf_tensor("arena", [128, ARN], F32))
        A = Arena(at, ARN)
        S = Sched(nc, stack)

        def ps(name, n, dt=F32):
            return Buf(name, stack.enter_context(nc.psum_tensor(name, [128, n], dt)))

        pA = [ps("pA0", 512), ps("pA1", 512)]
        pY = ps("pY", 2048)
        pS = ps("pS", 512)
        pT = ps("pT", 1024, BF16)

        ident_f = A.alloc("ident_f", [128, 128])
        ident_b = A.alloc("ident_b", [128, 128], BF16)
        ones_f = A.alloc("ones_f", [1, 128])
        bcol = A.alloc("bcol", [128, 40])
        bq8 = A.alloc("bq8", [128, 8])
        brow_in = A.alloc("brow_in", [1, 2048])
        brow_out = A.alloc("brow_out", [1, 2048])
        bs_row = A.alloc("bs_row", [1, 1024])
        small = A.alloc("small", [128, 64])
        persist_mark = A.off

        cst = Buf("cst", None)
        cst2 = Buf("cst2", None)

        def cast_dma(dst, src, trk=None):
            S.dma("pool", lambda e, dst=dst, src=src: e.dma_start(out=dst, in_=src), wr=(trk or cst))

        for r in range(16):
            cast_dma(w_in_bf[r * 128:(r + 1) * 128, :], w_in[r * 128:(r + 1) * 128, :])
        for r in range(16):
            cast_dma(xT_bf[r * 128:(r + 1) * 128, :], xT[r * 128:(r + 1) * 128, :])
        cast_dma(wsT_bf[:, :], wsT_d[:, :])
        cast_dma(skT_bf[:, :], skT_d[:, :])
        cast_dma(ident_bfd[:, :], ident_d[:, :])
        for c in range(5):
            cast_dma(bias_bf[c], bias_d[c], cst2)
        for r in range(4):
            cast_dma(w_out_bf[r * 512:(r + 1) * 512, :].rearrange("(a p) n -> p a n", p=128),
                     w_out[r * 512:(r + 1) * 512, :].rearrange("(a p) n -> p a n", p=128), cst2)
            cast_dma(wq_bf[r * 512:(r + 1) * 512, :].rearrange("(a p) n -> p a n", p=128),
                     wq[r * 512:(r + 1) * 512, :].rearrange("(a p) n -> p a n", p=128), cst2)
        for r in range(16):
            cast_dma(uT_bf[r * 128:(r + 1) * 128, :], uT[r * 128:(r + 1) * 128, :], cst2)
        for r in range(16):
            cast_dma(v_bf[r * 1024:(r + 1) * 1024, :].rearrange("(a p) n -> p a n", p=128),
                     vtab[r * 1024:(r + 1) * 1024, :].rearrange("(a p) n -> p a n", p=128), cst2)

        def ld(buf, dst, src, eng="sp", extra=()):
            S.dma(eng, lambda e, dst=dst, src=src: e.dma_start(out=dst, in_=src), wr=buf, extra=extra)

        def st(buf, dst, src, eng="sp"):
            S.dma(eng, lambda e, dst=dst, src=src: e.dma_start(out=dst, in_=src), rd=buf)

        ld(ident_f, ident_f[:, :], ident_d[:, :])
        ld(bcol, bcol[:, :], bcol_d[:, :])
        ld(brow_in, brow_in[:, :], brow_in_d[:, :])
        ld(brow_out, brow_out[:, :], brow_out_d[:, :])
        ld(bs_row, bs_row[:, :], bs_row_d[:, :])
        ld(ident_b, ident_b[:, :], ident_bfd[:, :], extra=[cst])
        S.op("dve", lambda e: e.memset(ones_f[:, :], 1.0), writes=[ones_f])
        S.op("dve", lambda e: e.tensor_scalar_mul(bq8[:, :], bcol[:, 16:24], 0.125),
             reads=[bcol], writes=[bq8])

        def mm(out, lhsT, rhs, start, stop, reads, writes):
            S.op("pe", lambda e, o=out, l=lhsT, r=rhs, s0=start, s1=stop:
                 e.matmul(o, lhsT=l, rhs=r, start=s0, stop=s1), reads=reads, writes=writes)

        def act(out, in_, func, reads, writes, bias=None, scale=None, ww_ok=False):
            kw = {}
            if bias is not None:
                kw["bias"] = bias
            if scale is not None:
                kw["scale"] = scale
            S.op("act", lambda e, o=out, i=in_, f=func, kw=kw: e.activation(o, i, f, **kw),
                 reads=reads, writes=writes, ww_ok=ww_ok)

        def dve(fn, reads, writes, eng="dve", ww_ok=False):
            S.op(eng, fn, reads=reads, writes=writes, ww_ok=ww_ok)

        def layernorm(src, nfree, g_bc, b_bc, dst, scr):
            nch = nfree // 512
            sAP, sB = src
            dAP, dB = dst
            for c in range(nch):
                dve(lambda e, c=c: e.bn_stats(scr[:, c * 6:(c + 1) * 6], sAP[:, c * 512:(c + 1) * 512]),
                    [sB, scr], [scr])
            dve(lambda e: e.bn_aggr(scr[:, 32:34], scr[:, 0:nch * 6]), [scr], [scr])
            dve(lambda e: e.tensor_scalar_add(scr[:, 34:35], scr[:, 33:34], LN_EPS), [scr], [scr])
            act(scr[:, 35:36], scr[:, 34:35], AF.Ln, [scr], [scr])
            act(scr[:, 36:37], scr[:, 35:36], AF.Exp, [scr], [scr], scale=-0.5)
            dve(lambda e: e.tensor_scalar(out=sAP, in0=sAP, scalar1=scr[:, 32:33], scalar2=scr[:, 36:37],
                                          op0=ALU.subtract, op1=ALU.mult), [sB, scr], [sB])
            gAP, gB = g_bc if isinstance(g_bc, tuple) else (g_bc[:, :], g_bc)
            bAP, bB = b_bc if isinstance(b_bc, tuple) else (b_bc[:, :], b_bc)
            gBl = gB if isinstance(gB, list) else [gB]
            bBl = bB if isinstance(bB, list) else [bB]
            dve(lambda e: e.tensor_tensor(out=sAP, in0=sAP, in1=gAP, op=ALU.mult), [sB] + gBl, [sB])
            dve(lambda e: e.tensor_tensor(out=dAP, in0=sAP, in1=bAP, op=ALU.add), [sB] + bBl, [dB])

        W = [A.alloc("W0", [128, 16, 512], BF16), A.alloc("W1", [128, 16, 512], BF16)]
        mark_w = A.off
        XB = [A.alloc("XB0", [128, 16, 512], BF16), A.alloc("XB1", [128, 16, 512], BF16)]
        stage_mark = A.off
        W1s = W + [A.alloc("W2", [128, 16, 512], BF16), A.alloc("W3", [128, 16, 512], BF16)]
        uTb = A.alloc("uTb", [128, 8, 512])
        vg = A.alloc("vg", [128, 4, 1024])
        vng = A.alloc("vng", [128, 1024])
        vnb = A.alloc("vnb", [128, 1024])
        vn = A.alloc("vn", [128, 1024], BF16)
        vaug = [A.alloc("vaug%d" % i, [128, 16, 65], BF16) for i in range(4)]
        qk = [A.alloc("qk0", [128, 512], BF16), A.alloc("qk1", [128, 512], BF16)]
        ATb = A.alloc("ATb", [128, 8, 128], BF16)
        wsT = A.alloc("wsT", [128, 8, 128], BF16)
        ld(vng, vng[:, :], vng_d[:, :])
        ld(vnb, vnb[:, :], vnb_d[:, :])
        ld(wsT, wsT[:, :, :], wsT_bf.rearrange("p (g q) -> p g q", g=8), extra=[cst])
        for i in range(4):
            dve(lambda e, i=i: e.memset(vaug[i][:, :, 64:65], 1.0), [], [vaug[i]])

        w_in_v = w_in_bf.rearrange("(kc p) n -> p kc n", p=128)
        xT_v = xT_bf.rearrange("(kc p) n -> p kc n", p=128)
        qT_v = qT_s.rearrange("(j p) n -> p j n", p=128)
        kT_v = kT_s.rearrange("(j p) n -> p j n", p=128)
        AT_v = AT_s.rearrange("(g p) n -> p g n", p=128)
        wcount = 0
        pacount = 0
        qkcount = 0
        NBLK1 = 9
        for blk in range(NBLK1):
            xb = XB[blk % 2]
            for kq in range(4):
                ld(xb, xb[:, kq * 4:(kq + 1) * 4, :], xT_v[:, kq * 4:(kq + 1) * 4, blk * 512:(blk + 1) * 512],
                   extra=[cst])
            for cb in range(10):
                wb = W1s[wcount % 4]
                wcount += 1
                for kq in range(4):
                    ld(wb, wb[:, kq * 4:(kq + 1) * 4, :], w_in_v[:, kq * 4:(kq + 1) * 4, cb * 512:(cb + 1) * 512],
                       extra=[cst])
                if cb in (0, 1, 4, 5, 6, 7):
                    for g in range(4):
                        pa = pA[pacount % 2]
                        pacount += 1
                        for kc in range(16):
                            mm(pa[:, :], wb[:, kc, g * 128:(g + 1) * 128], xb[:, kc, :], kc == 0, kc == 15,
                               [wb, xb], [pa])
                        if cb < 2:
                            gi = cb * 4 + g
                            act(uTb[:, gi, :], pa[:, :], AF.Gelu, [pa, bcol], [uTb], bias=bcol[:, gi:gi + 1])
                        else:
                            j = (cb - 4) * 4 + g if cb < 6 else (cb - 6) * 4 + g
                            ob = qk[qkcount % 2]
                            qkcount += 1
                            if cb < 6:
                                act(ob[:, :], pa[:, :], AF.Identity, [pa, bq8], [ob], bias=bq8[:, j:j + 1], scale=0.125)
                                st(ob, qT_v[:, j, blk * 512:(blk + 1) * 512], ob[:, :])
                            else:
                                act(ob[:, :], pa[:, :], AF.Identity, [pa, bcol], [ob], bias=bcol[:, 24 + j:25 + j])
                                st(ob, kT_v[:, j, blk * 512:(blk + 1) * 512], ob[:, :])
                else:
                    for tt in range(4):
                        pa = pA[pacount % 2]
                        pacount += 1
                        boff = (cb - 2) * 512 if cb < 4 else 1024 + (cb - 8) * 512
                        mm(pa[:, :], ones_f[0:1, :], brow_in[0:1, boff:boff + 512], True, False,
                           [ones_f, brow_in], [pa])
                        for kc in range(16):
                            mm(pa[:, :], xb[:, kc, tt * 128:(tt + 1) * 128], wb[:, kc, :], False, kc == 15,
                               [wb, xb], [pa])
                        if cb < 4:
                            act(vg[:, tt, (cb - 2) * 512:(cb - 1) * 512], pa[:, :], AF.Gelu, [pa], [vg])
                        else:
                            h0 = (cb - 8) * 8
                            act(vaug[tt][:, h0:h0 + 8, 0:64], pa[:, :].rearrange("p (h d) -> p h d", h=8),
                                AF.Copy, [pa], [vaug[tt]])
            for tt in range(4):
                et = blk * 4 + tt
                st(vaug[tt], vaug_s[et * 128:(et + 1) * 128, :], vaug[tt][:, :, :].rearrange("p h d -> p (h d)"))
                if et < 2 or et >= 34:
                    continue
                t = et - 2
                vgt = vg[:, tt, :]
                layernorm((vgt, vg), 1024, vng, vnb, (vn[:, :], vn), small)
                for g in range(8):
                    mm(pY[:, g * 128:(g + 1) * 128], ones_f[0:1, :], bs_row[0:1, g * 128:(g + 1) * 128], True, False,
                       [ones_f, bs_row], [pY])
                    mm(pY[:, g * 128:(g + 1) * 128], vn[:, g * 128:(g + 1) * 128], wsT[:, g, :], False, True,
                       [vn, wsT], [pY])
                dve(lambda e, tt=tt: e.tensor_tensor(out=ATb[:, :, :],
                                                     in0=pY[:, 0:1024].rearrange("p (g q) -> p g q", g=8),
                                                     in1=uTb[:, :, tt * 128:(tt + 1) * 128], op=ALU.mult),
                    [pY, uTb], [ATb])
                st(ATb, AT_v[:, :, t * 128:(t + 1) * 128], ATb[:, :, :])

        S.barrier()
        A.off = persist_mark
        WO = A.alloc("WO", [128, 16, 2048], BF16)
        biasT = A.alloc("biasT", [128, 16, 5, 128], BF16)
        kTw = A.alloc("kTw", [128, 8, 640], BF16)
        vaw = A.alloc("vaw", [128, 5, 1040], BF16)
        qTt = A.alloc("qTt", [128, 8, 128], BF16)
        PT = A.alloc("PT", [128, 640], BF16)
        Bb = A.alloc("Bb", [128, 1024], BF16)
        BT = A.alloc("BT", [128, 8, 128], BF16)
        ATt = A.alloc("ATt", [128, 8, 128], BF16)
        xnb = A.alloc("xnb", [128, 2048], BF16)
        xnT = A.alloc("xnT", [128, 16, 128], BF16)
        xt = A.alloc("xt", [128, 2048])
        rr = A.alloc("rr", [128, 2048])
        ln1g = A.alloc("ln1g", [128, 2048])
        ln1b = A.alloc("ln1b", [128, 2048])
        rec = A.alloc("rec", [128, 8])
        ld(ln1g, ln1g[:, :], ln1g_d[:, :])
        ld(ln1b, ln1b[:, :], ln1b_d[:, :])
        wo_v = w_out_bf.rearrange("(kc p) n -> p kc n", p=128)
        for kq in range(8):
            ld(WO, WO[:, kq * 2:(kq + 1) * 2, :], wo_v[:, kq * 2:(kq + 1) * 2, :])
        vaug_v = vaug_s.rearrange("(e p) f -> p e f", p=128)
        xnT_v = xnT_s.rearrange("(kc p) n -> p kc n", p=128)
        loaded_cls = -1
        NT = 32
        for t in range(NT):
            cls = 0 if t == 0 else 1 if t == 1 else 3 if t == 30 else 4 if t == 31 else 2
            if cls != loaded_cls:
                bv = bias_bf[cls].rearrange("p (h c q) -> p h c q", h=16, c=5)
                for hq in range(4):
                    ld(biasT, biasT[:, hq * 4:(hq + 1) * 4, :, :], bv[:, hq * 4:(hq + 1) * 4, :, :])
                loaded_cls = cls
            ld(qTt, qTt[:, :, :], qT_v[:, :, (t + 2) * 128:(t + 3) * 128])
            ld(kTw, kTw[:, :, :], kT_v[:, :, t * 128:t * 128 + 640])
            ld(vaw, vaw[:, :, :], vaug_v[:, t:t + 5, :])
            ld(ATt, ATt[:, :, :], AT_v[:, :, t * 128:(t + 1) * 128])
            ld(xt, xt[:, :], xtok[t * 128:(t + 1) * 128, :])
            for h in range(16):
                j, base = h // 2, 64 * (h % 2)
                for ck in range(5):
                    dst, dB = (pS[:, ck * 128:(ck + 1) * 128], pS) if ck < 4 else (pA[0][:, 0:128], pA[0])
                    mm(dst, kTw[base:base + 64, j, ck * 128:(ck + 1) * 128], qTt[base:base + 64, j, :], True, False,
                       [kTw, qTt], [dB])
                    mm(dst, ident_b[:, :], biasT[:, h, ck, :], False, True, [ident_b, biasT], [dB])
                act(PT[:, 0:512], pS[:, :], AF.Exp, [pS], [PT])
                act(PT[:, 512:640], pA[0][:, 0:128], AF.Exp, [pA[0]], [PT])
                for ck in range(5):
                    mm(pA[1][:, 0:65], PT[:, ck * 128:(ck + 1) * 128], vaw[:, ck, h * 65:(h + 1) * 65], ck == 0, ck == 4,
                       [PT, vaw], [pA[1]])
                dve(lambda e: e.reciprocal(rec[:, 0:1], pA[1][:, 64:65]), [pA[1]], [rec])
                dve(lambda e, h=h: e.tensor_scalar_mul(Bb[:, h * 64:(h + 1) * 64], pA[1][:, 0:64], rec[:, 0:1]),
                    [pA[1], rec], [Bb])
            for j in range(8):
                S.op("pe", lambda e, j=j: e.transpose(pT[:, j * 128:(j + 1) * 128], Bb[:, j * 128:(j + 1) * 128],
                                                      ident_b[:, :]), reads=[Bb, ident_b], writes=[pT])
            dve(lambda e: e.tensor_copy(BT[:, :, :], pT[:, :].rearrange("p (j q) -> p j q", j=8)), [pT], [BT])
            for nb in range(4):
                dst = pY[:, nb * 512:(nb + 1) * 512]
                mm(dst, ones_f[0:1, :], brow_out[0:1, nb * 512:(nb + 1) * 512], True, False, [ones_f, brow_out], [pY])
                for g in range(8):
                    mm(dst, ATt[:, g, :], WO[:, g, nb * 512:(nb + 1) * 512], False, False, [ATt, WO], [pY])
                for j in range(8):
                    mm(dst, BT[:, j, :], WO[:, 8 + j, nb * 512:(nb + 1) * 512], False, j == 7, [BT, WO], [pY])
            dve(lambda e: e.scalar_tensor_tensor(out=rr[:, :], in0=xt[:, :], scalar=ALPHA, in1=pY[:, :],
                                                 op0=ALU.mult, op1=ALU.add), [xt, pY], [rr])
            layernorm((rr[:, :], rr), 2048, ln1g, ln1b, (rr[:, :], rr), small)
            st(rr, xn_s[t * 128:(t + 1) * 128, :], rr[:, :])
            act(xnb[:, :], rr[:, :], AF.Copy, [rr], [xnb])
            for half in range(2):
                for j in range(8):
                    kc = half * 8 + j
                    S.op("pe", lambda e, j=j, kc=kc: e.transpose(pT[:, j * 128:(j + 1) * 128],
                                                                 xnb[:, kc * 128:(kc + 1) * 128], ident_b[:, :]),
                         reads=[xnb, ident_b], writes=[pT])
                dve(lambda e, half=half: e.tensor_copy(xnT[:, half * 8:(half + 1) * 8, :],
                                                       pT[:, :].rearrange("p (j q) -> p j q", j=8)), [pT], [xnT])
            st(xnT, xnT_v[:, :, t * 128:(t + 1) * 128], xnT[:, :, :])

        S.barrier()
        A.off = mark_w
        TB = 256
        NTT = TB // 128
        XB3 = [A.alloc("XB3_0", [128, 16, TB], BF16), A.alloc("XB3_1", [128, 16, TB], BF16)]
        V = [A.alloc("V0", [128, 4, 2048], BF16), A.alloc("V1", [128, 4, 2048], BF16)]
        qpT = V[0]
        qpT_ap = V[0][:, :, :].rearrange("p a b -> p (a b)").rearrange("p (c t) -> p c t", c=16)
        skT = A.alloc("skT", [128, 16, 128], BF16)
        mg_off = A.off
        MG = [A.alloc("MG%d" % i, [128, 2, 512], BF16) for i in range(4)]
        mk_off = A.off
        MK = [A.alloc("MK%d" % i, [128, 2, 512], BF16) for i in range(4)]
        assert mk_off - mg_off == 2048 and A.off - mk_off == 2048
        ln2g_ap = at[:, mg_off:mg_off + 2048]
        ln2b_ap = at[:, mk_off:mk_off + 2048]
        PB = [A.alloc("PB%d" % i, [128, 2, 512], BF16) for i in range(2)]
        PG = [A.alloc("PG%d" % i, [128, 2, 4, 128]) for i in range(2)]
        r_sb = A.alloc("r_sb", [128, NTT, 16, 128])
        Wt = A.alloc("Wt", [128, 512], BF16)
        WTb = [A.alloc("WTb0", [128, 512], BF16), A.alloc("WTb1", [128, 512], BF16)]
        gH = [A.alloc("gH0", [128, 512]), A.alloc("gH1", [128, 512])]
        s_sb = A.alloc("s_sb", [128, NTT, 16, 128])
        yacc = A.alloc("yacc", [128, NTT, 2048])
        xres = A.alloc("xres", [128, 2048])
        top = A.alloc("top", [128, 16, 16])
        cand = A.alloc("cand", [128, 4, 256])
        c16 = A.alloc("c16", [128, 8, 16])
        e16 = A.alloc("e16", [128, 8, 16])
        tmp = A.alloc("tmp", [128, 256])
        sm3 = A.alloc("sm3", [128, 64])
        biasg = A.alloc("biasg", [128, NTT, 8])
        cth = A.alloc("cth", [128, NTT, 8])
        ld(skT, skT[:, :, :], skT_bf.rearrange("p (c n) -> p c n", c=16))
        print("stage3 arena", A.off, A.n)
        wq_v = wq_bf.rearrange("(kc p) n -> p kc n", p=128)
        uT_v = uT_bf.rearrange("(kc p) n -> p kc n", p=128)
        NEB = 32
        NIT = NEB * NTT
        for tb in range(4096 // TB):
            xb = XB3[tb % 2]
            for kq in range(4):
                ld(xb, xb[:, kq * 4:(kq + 1) * 4, :], xnT_v[:, kq * 4:(kq + 1) * 4, tb * TB:(tb + 1) * TB])
            for cbq in range(4):
                wb = W[wcount % 2]
                wcount += 1
                for kq in range(4):
                    ld(wb, wb[:, kq * 4:(kq + 1) * 4, :], wq_v[:, kq * 4:(kq + 1) * 4, cbq * 512:(cbq + 1) * 512])
                for g in range(4):
                    c = cbq * 4 + g
                    pa = pA[pacount % 2]
                    pacount += 1
                    for kc in range(16):
                        mm(pa[:, 0:TB], wb[:, kc, g * 128:(g + 1) * 128], xb[:, kc, :], kc == 0, kc == 15,
                           [wb, xb], [pa])
                    act(qpT_ap[:, c, 0:TB], pa[:, 0:TB], AF.Copy, [pa], [qpT])
            for tt in range(NTT):
                for cg in range(4):
                    pa = pA[pacount % 2]
                    pacount += 1
                    for ci in range(4):
                        c = cg * 4 + ci
                        mm(pa[:, ci * 128:(ci + 1) * 128], qpT_ap[:, c, tt * 128:(tt + 1) * 128], skT[:, c, :],
                           True, True, [qpT, skT], [pa])
                    dve(lambda e, tt=tt, cg=cg, pa=pa: e.tensor_copy(
                        s_sb[:, tt, cg * 4:(cg + 1) * 4, :], pa[:, :].rearrange("p (c n) -> p c n", c=4)),
                        [pa], [s_sb])
                for c in range(16):
                    dve(lambda e, tt=tt, c=c: e.max(out=top[:, c, 0:8], in_=s_sb[:, tt, c, :]), [s_sb], [top])
                    dve(lambda e, tt=tt, c=c: e.match_replace(out=tmp[:, 0:128], in_to_replace=top[:, c, 0:8],
                                                              in_values=s_sb[:, tt, c, :], imm_value=-1e30),
                        [s_sb, top], [tmp])
                    dve(lambda e, c=c: e.max(out=top[:, c, 8:16], in_=tmp[:, 0:128]), [tmp], [top])
                topv = top[:, :, :].rearrange("p (h two) k -> p h two k", two=2)
                for hh in range(2):
                    hs = slice(hh * 4, hh * 4 + 4)
                    dve(lambda e, hs=hs: e.tensor_tensor(
                        out=cand[:, :, :].rearrange("p h (a b) -> p h a b", a=16),
                        in0=topv[:, hs, 0, :].unsqueeze(3).to_broadcast([128, 4, 16, 16]),
                        in1=topv[:, hs, 1, :].unsqueeze(2).to_broadcast([128, 4, 16, 16]), op=ALU.add),
                        [top], [cand])
                    for h4 in range(4):
                        h = hh * 4 + h4
                        dve(lambda e, h=h, h4=h4: e.max(out=c16[:, h, 0:8], in_=cand[:, h4, :]), [cand], [c16])
                        dve(lambda e, h=h, h4=h4: e.match_replace(out=tmp[:, :], in_to_replace=c16[:, h, 0:8],
                                                                  in_values=cand[:, h4, :], imm_value=-1e30),
                            [cand, c16], [tmp])
                        dve(lambda e, h=h: e.max(out=c16[:, h, 8:16], in_=tmp[:, :]), [tmp], [c16])
                dve(lambda e: e.tensor_scalar_add(sm3[:, 0:8], c16[:, :, 15], -TAU_EPS), [c16], [sm3])
                dve(lambda e: e.tensor_tensor(out=e16[:, :, :], in0=c16[:, :, :],
                                              in1=c16[:, :, 0:1].to_broadcast([128, 8, 16]), op=ALU.subtract),
                    [c16], [e16])
                act(e16[:, :, :], e16[:, :, :], AF.Exp, [e16], [e16])
                dve(lambda e: e.reduce_sum(out=sm3[:, 8:16], in_=e16[:, :, :], axis=AX.X), [e16, sm3], [sm3])
                act(sm3[:, 16:24], sm3[:, 8:16], AF.Ln, [sm3], [sm3])
                dve(lambda e: e.tensor_tensor(out=sm3[:, 24:32], in0=sm3[:, 0:8], in1=c16[:, :, 0], op=ALU.subtract),
                    [sm3, c16], [sm3])
                dve(lambda e, tt=tt: e.tensor_tensor(out=biasg[:, tt, :], in0=sm3[:, 24:32], in1=sm3[:, 16:24],
                                                     op=ALU.subtract), [sm3], [biasg])
                act(cth[:, tt, :], biasg[:, tt, :], AF.Exp, [biasg], [cth])
                dve(lambda e, tt=tt: e.tensor_scalar_mul(cth[:, tt, :], cth[:, tt, :], -1.0), [cth], [cth])
                dve(lambda e, tt=tt, topv=topv: e.tensor_tensor(out=sm3[:, 32:40], in0=biasg[:, tt, :],
                                                                in1=topv[:, :, 1, 0], op=ALU.add),
                    [biasg, top, sm3], [sm3])
                dve(lambda e: e.tensor_tensor(out=sm3[:, 32:40], in0=sm3[:, 32:40], in1=sm3[:, 0:8],
                                              op=ALU.subtract), [sm3], [sm3])
                sv = s_sb[:, tt, :, :].rearrange("p (h two) n -> p h two n", two=2)
                rv = r_sb[:, tt, :, :].rearrange("p (h two) n -> p h two n", two=2)
                dve(lambda e, sv=sv, rv=rv: e.tensor_tensor(
                    out=rv[:, :, 0, :], in0=sm3[:, 0:8].unsqueeze(2).to_broadcast([128, 8, 128]),
                    in1=sv[:, :, 0, :], op=ALU.subtract), [s_sb, sm3], [r_sb])
                dve(lambda e, sv=sv, rv=rv: e.tensor_copy(rv[:, :, 1, :], sv[:, :, 1, :]), [s_sb], [r_sb])
                dve(lambda e, sv=sv: e.tensor_tensor(out=sv[:, :, 0, :], in0=sv[:, :, 0, :],
                                                     in1=sm3[:, 32:40].unsqueeze(2).to_broadcast([128, 8, 128]),
                                                     op=ALU.add), [s_sb, sm3], [s_sb])
                dve(lambda e, sv=sv, topv=topv: e.tensor_tensor(
                    out=sv[:, :, 1, :], in0=sv[:, :, 1, :],
                    in1=topv[:, :, 1, 0:1].to_broadcast([128, 8, 128]), op=ALU.subtract),
                    [s_sb, top], [s_sb])
                act(s_sb[:, tt, :, :], s_sb[:, tt, :, :], AF.Exp, [s_sb], [s_sb])

            wbase = wcount
            wcount += NEB

            def load_W(eb):
                wb = W[(wbase + eb) % 2]
                for kq in range(4):
                    ld(wb, wb[:, kq * 4:(kq + 1) * 4, :], uT_v[:, kq * 4:(kq + 1) * 4, eb * 512:(eb + 1) * 512])

            def load_V(eb):
                vb = V[eb % 2]
                vsrc = v_bf[eb * 512:(eb + 1) * 512, :].rearrange("(c p) n -> p c n", p=128)
                for c in range(4):
                    ld(vb, vb[:, c, :], vsrc[:, c, :])

            def st_P(it):
                eb, tt = divmod(it, NTT)
                i0 = eb * 4
                rv = r_sb[:, tt, :, :].rearrange("p (h two) n -> p h two n", two=2)
                sv = s_sb[:, tt, :, :].rearrange("p (h two) n -> p h two n", two=2)
                for sg in (2, 0, 3, 1):
                    hs = slice(2 * sg, 2 * sg + 2)
                    if sg < 2:
                        dve(lambda e, rv=rv, i0=i0, sg=sg, hs=hs: e.tensor_tensor(
                            out=MK[sg][:, :, :].rearrange("p h (a j) -> p h a j", a=4),
                            in0=rv[:, hs, 1, :].unsqueeze(2).to_broadcast([128, 2, 4, 128]),
                            in1=rv[:, hs, 0, i0:i0 + 4].unsqueeze(3).to_broadcast([128, 2, 4, 128]),
                            op=ALU.subtract), [r_sb], [MK[sg]], eng="pool")
                    else:
                        dve(lambda e, sv=sv, i0=i0, sg=sg, hs=hs: e.tensor_tensor(
                            out=PG[sg - 2][:, :, :, :],
                            in0=sv[:, hs, 0, i0:i0 + 4].unsqueeze(3).to_broadcast([128, 2, 4, 128]),
                            in1=sv[:, hs, 1, :].unsqueeze(2).to_broadcast([128, 2, 4, 128]),
                            op=ALU.mult), [s_sb], [PG[sg - 2]], eng="pool")

            def st_PB(it):
                eb, tt = divmod(it, NTT)
                i0 = eb * 4
                sv = s_sb[:, tt, :, :].rearrange("p (h two) n -> p h two n", two=2)
                for sg in (0, 2, 1, 3):
                    for hh in range(2):
                        h = 2 * sg + hh
                        if sg < 2:
                            for a in range(4):
                                act(PB[sg][:, hh, a * 128:(a + 1) * 128], sv[:, h, 1, :], AF.Copy, [s_sb], [PB[sg]],
                                    scale=sv[:, h, 0, i0 + a:i0 + a + 1], ww_ok=True)
                        else:
                            pv = PG[sg - 2][:, hh, :, :].rearrange("p a j -> p (a j)")
                            act(MK[sg][:, hh, :], pv, AF.Sign, [PG[sg - 2], cth], [MK[sg]],
                                bias=cth[:, tt, h:h + 1], ww_ok=True)

            def st_M(it):
                for sg in (2, 0, 3, 1):
                    for hh in range(2):
                        if sg < 2:
                            dve(lambda e, sg=sg, hh=hh: e.scalar_tensor_tensor(
                                out=MG[sg][:, hh, :], in0=MK[sg][:, hh, :], scalar=0.0, in1=PB[sg][:, hh, :],
                                op0=ALU.is_ge, op1=ALU.mult), [MK[sg], PB[sg]], [MG[sg]], ww_ok=True)
                        else:
                            pv = PG[sg - 2][:, hh, :, :].rearrange("p a j -> p (a j)")
                            dve(lambda e, sg=sg, hh=hh, pv=pv: e.scalar_tensor_tensor(
                                out=MG[sg][:, hh, :], in0=MK[sg][:, hh, :], scalar=0.0, in1=pv,
                                op0=ALU.max, op1=ALU.mult), [MK[sg], PG[sg - 2]], [MG[sg]], ww_ok=True)

            def st_Gs(it):
                for h in range(8):
                    mm(pS[:, :], ident_b[:, :], MG[h // 2][:, h % 2, :], h == 0, h == 7, [ident_b, MG[h // 2]], [pS])

            def st_H(it):
                eb, tt = divmod(it, NTT)
                wb = W[(wbase + eb) % 2]
                pa = pA[it % 2]
                for kc in range(16):
                    mm(pa[:, :], xb[:, kc, tt * 128:(tt + 1) * 128], wb[:, kc, :], kc == 0, kc == 15, [wb, xb], [pa])

            def st_gelu(it):
                act(gH[it % 2][:, :], pA[it % 2][:, :], AF.Gelu, [pA[it % 2]], [gH[it % 2]])

            def st_W(it):
                g_ = gH[it % 2]
                dve(lambda e, g_=g_: e.tensor_tensor(out=Wt[:, :], in0=g_[:, :], in1=pS[:, :], op=ALU.mult),
                    [g_, pS], [Wt])

            def st_T(it):
                for c in range(4):
                    S.op("pe", lambda e, c=c: e.transpose(pT[:, c * 128:(c + 1) * 128],
                                                          Wt[:, c * 128:(c + 1) * 128], ident_b[:, :]),
                         reads=[Wt, ident_b], writes=[pT])

            def st_WTc(it):
                act(WTb[it % 2][:, :], pT[:, 0:512], AF.Copy, [pT], [WTb[it % 2]])

            def st_y(it):
                eb, tt = divmod(it, NTT)
                vb = V[eb % 2]
                wt_ = WTb[it % 2]
                for nb in range(4):
                    for c in range(4):
                        mm(pY[:, nb * 512:(nb + 1) * 512], wt_[:, c * 128:(c + 1) * 128],
                           vb[:, c, nb * 512:(nb + 1) * 512], c == 0, c == 3, [wt_, vb], [pY])

            def st_yacc(it):
                eb, tt = divmod(it, NTT)
                if eb == 0:
                    dve(lambda e, tt=tt: e.tensor_copy(yacc[:, tt, :], pY[:, :]), [pY], [yacc])
                else:
                    dve(lambda e, tt=tt: e.tensor_tensor(out=yacc[:, tt, :], in0=yacc[:, tt, :], in1=pY[:, :],
                                                         op=ALU.add), [pY, yacc], [yacc])

            load_W(0)
            load_W(1)
            load_V(0)
            load_V(1)
            st_P(0)
            st_PB(0)
            st_M(0)
            st_H(0)
            st_gelu(0)
            for r in range(NIT + 2):
                if r < NIT:
                    st_Gs(r)
                if r + 1 < NIT:
                    if (r + 1) % NTT == 0:
                        ebn = (r + 1) // NTT + 1
                        if ebn < NEB:
                            load_W(ebn)
                    st_P(r + 1)
                    st_PB(r + 1)
                    st_H(r + 1)
                if r >= 2:
                    st_yacc(r - 2)
                if r < NIT:
                    st_W(r)
                    st_T(r)
                if r + 1 < NIT:
                    st_M(r + 1)
                    st_gelu(r + 1)
                if r < NIT:
                    st_WTc(r)
                if 1 <= r <= NIT:
                    st_y(r - 1)
                if r >= NTT and r % NTT == 0 and r // NTT + 1 < NEB:
                    load_V(r // NTT + 1)
            for i in range(4):
                ld(MG[i], ln2g_ap[:, i * 512:(i + 1) * 512], ln2g_d[:, i * 512:(i + 1) * 512])
                ld(MK[i], ln2b_ap[:, i * 512:(i + 1) * 512], ln2b_d[:, i * 512:(i + 1) * 512])
            for tt in range(NTT):
                t = tb * NTT + tt
                ld(xres, xres[:, :], xn_s[t * 128:(t + 1) * 128, :])
                dve(lambda e, tt=tt: e.scalar_tensor_tensor(out=xres[:, :], in0=xres[:, :], scalar=ALPHA,
                                                            in1=yacc[:, tt, :], op0=ALU.mult, op1=ALU.add),
                    [xres, yacc], [xres])
                layernorm((xres[:, :], xres), 2048, (ln2g_ap, list(MG)), (ln2b_ap, list(MK)), (xres[:, :], xres), sm3)
                st(xres, outp[t * 128:(t + 1) * 128, :], xres[:, :])
            S.barrier()

        S.barrier()
        S.finalize()
        S.check()
        with nc.Block() as block:
            S.emit(block)
    return nc


def _bias_tables(rpb, q):
    out = np.full((5, 128, 16, 5, 128), NEG, np.float32)
    qc = np.arange(64)
    cs = np.clip(qc - 8, 0, 48)
    kc = np.arange(64)
    colok = (kc[:, None] >= cs[None, :]) & (kc[:, None] < cs[None, :] + 16)
    coff = kc[:, None] - qc[None, :] + 15
    for ci, t in enumerate((0, 1, 2, 30, 31)):
        for qp in range(2):
            R = 64 * q + 2 * t + qp
            rs = min(max(R - 4, 0), 248)
            seen = set()
            order = list(range(5))
            slots = []
            for ck in order:
                for kp in range(2):
                    lr = 2 * t - 4 + 2 * ck + kp
                    wrapped = (q == 0 and lr < 0) or (q == 3 and lr >= 64)
                    KR = 64 * q + lr + (8 if (q == 0 and lr < 0) else 0) - (8 if (q == 3 and lr >= 64) else 0)
                    slots.append((wrapped, ck, kp, KR))
            slots.sort(key=lambda s: s[0])
            for wrapped, ck, kp, KR in slots:
                if KR < rs or KR >= rs + 8 or KR in seen:
                    continue
                seen.add(KR)
                ro = KR - R + 7
                vals = rpb[:, ro, :][:, np.clip(coff, 0, 30)]
                vals = np.where(colok[None], vals, NEG)
                out[ci, kp * 64:(kp + 1) * 64, :, ck, qp * 64:(qp + 1) * 64] = vals.transpose(1, 0, 2)
            assert len(seen) == 8
    return out.reshape(5, 128, 16 * 5 * 128)


_NC_CACHE = {}


def kernel(x, w_in, b_in, v_norm_g, v_norm_b, w_spatial, b_spatial, rpb, w_out, b_out,
           ln1_g, ln1_b, peer_wq, peer_subkeys, peer_u, peer_v, ln2_g, ln2_b):
    f = lambda a: np.ascontiguousarray(np.asarray(a, dtype=np.float32))
    x = f(x)
    bc = lambda v, n: f(np.broadcast_to(np.asarray(v, np.float32).reshape(1, n), (128, n)))
    b_in0 = np.asarray(b_in, np.float32)[0]
    shared = {
        "w_in": f(w_in[0]), "w_out": f(w_out[0]), "wq": f(peer_wq[0]),
        "uT": f(np.asarray(peer_u[0]).T), "vtab": f(peer_v[0]),
        "bcol": f(b_in0.reshape(40, 128).T),
        "brow_in": f(np.concatenate([b_in0[1024:2048], b_in0[4096:5120]]).reshape(1, 2048)),
        "brow_out": f(np.asarray(b_out[0]).reshape(1, 2048)),
        "bs_row": f(np.asarray(b_spatial[0]).reshape(1, 1024)),
        "vng": bc(v_norm_g[0], 1024), "vnb": bc(v_norm_b[0], 1024),
        "ln1g": bc(ln1_g[0], 2048), "ln1b": bc(ln1_b[0], 2048),
        "ln2g": bc(ln2_g[0], 2048), "ln2b": bc(ln2_b[0], 2048),
        "wsT": f(np.asarray(w_spatial[0]).transpose(2, 0, 1).reshape(128, 1024)),
        "skT": f(np.asarray(peer_subkeys[0]).transpose(3, 0, 1, 2).reshape(128, 2048)),
        "ident": np.eye(128, dtype=np.float32),
    }
    rpb0 = np.asarray(rpb[0], np.float32)
    tabs = [_bias_tables(rpb0, q) for q in range(4)]
    in_maps = []
    for c in range(8):
        b, q = c // 4, c % 4
        xb = x[b]
        rows = []
        for lr in range(-4, 68):
            gr = 64 * q + lr
            if q == 0 and lr < 0:
                gr = lr + 8
            if q == 3 and lr >= 64:
                gr = 64 * q + lr - 8
            rows.append(gr)
        rows = np.asarray(rows)
        tok = (rows[:, None] * 64 + np.arange(64)[None, :]).reshape(-1)
        xext = xb[tok]
        m = dict(shared)
        m["xT"] = np.ascontiguousarray(xext.T)
        m["xtok"] = np.ascontiguousarray(xb[4096 * q:4096 * (q + 1)])
        m["biasT"] = tabs[q]
        in_maps.append(m)
    if "nc" not in _NC_CACHE:
        _NC_CACHE["nc"] = build()
    res = run_bass_kernel_spmd(_NC_CACHE["nc"], in_maps, core_ids=list(range(8)))
    out = np.empty((2, 16384, 2048), np.float32)
    for c in range(8):
        b, q = c // 4, c % 4
        out[b, 4096 * q:4096 * (q + 1)] = res.results[c]["out"]
    return out
```

```python
import numpy as np
from contextlib import ExitStack
import concourse.bass as bass
import concourse.mybir as mybir
from concourse.bass_utils import run_bass_kernel_spmd

F32, BF16 = mybir.dt.float32, mybir.dt.bfloat16
AF = mybir.ActivationFunctionType
ALU = mybir.AluOpType
AX = mybir.AxisListType

ALPHA = 2.0 ** 0.25
LN_EPS = 1e-5
NEG = -30000.0
TAU_EPS = 2e-4
ENGS = ("pe", "act", "dve", "pool", "sp")
EPOCH = 30000


class Buf:
    def __init__(self, name, t):
        self.name = name
        self.t = t
        self.last_w = None
        self.reads = []
        self.dsem = None
        self.dcount = 0

    def __getitem__(self, k):
        return self.t[k]


class Sched:
    def __init__(self, nc, stack):
        self.nc = nc
        self.stack = stack
        self.ops = {e: [] for e in ENGS}
        self.ecount = {e: 0 for e in ENGS}
        self.dbufs = []

    def _dsem(self, b):
        if b.dsem is None:
            b.dsem = self.stack.enter_context(self.nc.semaphore("d_" + b.name))
            self.dbufs.append(b)
        return b.dsem

    def _deps(self, reads, writes, ww_eng=None):
        deps = []
        for b in reads:
            if b.last_w is not None:
                deps.append(b.last_w)
        for b in writes:
            if b.last_w is not None:
                lw = b.last_w
                if not (ww_eng is not None and lw[0] == 'e' and lw[1] == ww_eng):
                    deps.append(lw)
            deps.extend(b.reads)
        return deps

    def op(self, eng, fn, reads=(), writes=(), ww_ok=False):
        deps = self._deps(reads, writes, eng if ww_ok else None)
        idx = self.ecount[eng]
        self.ecount[eng] += 1
        me = ('e', eng, idx)
        for b in reads:
            b.reads.append(me)
        for b in writes:
            b.last_w = me
            b.reads = []
        self.ops[eng].append([deps, fn, ('e', idx)])

    def dma(self, eng, fn, rd=None, wr=None, extra=()):
        b = rd if rd is not None else wr
        deps = self._deps([rd] if rd is not None else [], [wr] if wr is not None else [])
        for x in extra:
            if x.last_w is not None:
                deps.append(x.last_w)
        sem = self._dsem(b)
        b.dcount += 16
        me = ('d', b, b.dcount)
        if rd is not None:
            b.reads.append(me)
        else:
            b.last_w = me
            b.reads = []
        self.ops[eng].append([deps, fn, ('d', sem)])

    def barrier(self):
        deps = []
        for e in ENGS:
            if self.ecount[e] > 0:
                deps.append(('e', e, self.ecount[e] - 1))
        for b in self.dbufs:
            deps.append(('d', b, b.dcount))
        for e in ENGS:
            self.ops[e].append([list(deps), None, None])

    def finalize(self):
        signal = {e: set() for e in ENGS}
        for e in ENGS:
            view = {}
            for op in self.ops[e]:
                waits = []
                for d in op[0]:
                    if d[0] == 'e':
                        _, e2, idx = d
                        if e2 == e and e == "pe":
                            continue
                        if e2 == e and op[1] is None:
                            continue
                        key = ('e', e2)
                        if view.get(key, -1) >= idx:
                            continue
                        view[key] = idx
                        signal[e2].add(idx)
                        waits.append(d)
                    else:
                        _, b, cnt = d
                        key = ('d', b.name)
                        if view.get(key, 0) >= cnt:
                            continue
                        view[key] = cnt
                        waits.append(d)
                op[0] = waits
        self.rank = {}
        self.esems = {}
        for e in ENGS:
            srt = sorted(signal[e])
            self.rank[e] = {idx: r for r, idx in enumerate(srt)}
            nep = (len(srt) + EPOCH - 1) // EPOCH
            self.esems[e] = [self.stack.enter_context(self.nc.semaphore("s_%s%d" % (e, k)))
                             for k in range(max(nep, 1))]


    def check(self):
        pos = {e: 0 for e in ENGS}
        done_e = {e: -1 for e in ENGS}
        dcnt = {}
        total = sum(len(self.ops[e]) for e in ENGS)
        ndone = 0
        while ndone < total:
            progressed = False
            for e in ENGS:
                while pos[e] < len(self.ops[e]):
                    waits, fn, tag = self.ops[e][pos[e]]
                    ok = True
                    for d in waits:
                        if d[0] == 'e':
                            if done_e[d[1]] < d[2]:
                                ok = False
                                break
                        else:
                            if dcnt.get(d[1].name, 0) < d[2]:
                                ok = False
                                break
                    if not ok:
                        break
                    if tag is not None:
                        if tag[0] == 'e':
                            done_e[e] = tag[1]
                        else:
                            nm = [b for b in self.dbufs if b.dsem is tag[1]][0].name
                            dcnt[nm] = dcnt.get(nm, 0) + 16
                    pos[e] += 1
                    ndone += 1
                    progressed = True
            if not progressed:
                msg = []
                for e in ENGS:
                    if pos[e] < len(self.ops[e]):
                        waits = self.ops[e][pos[e]][0]
                        msg.append((e, pos[e], len(self.ops[e]), [(d[0], d[1] if d[0] == 'e' else d[1].name, d[2]) for d in waits], dict(done_e)))
                raise RuntimeError("DEADLOCK: %r" % (msg,))
        return True

    def emit(self, block):
        sched = self

        def run(eng_name):
            def body(e):
                for waits, fn, tag in sched.ops[eng_name]:
                    for d in waits:
                        if d[0] == 'e':
                            r = sched.rank[d[1]][d[2]]
                            e.wait_ge(sched.esems[d[1]][r // EPOCH], r % EPOCH + 1)
                        else:
                            e.wait_ge(d[1].dsem, d[2])
                    if fn is None:
                        continue
                    ins = fn(e)
                    if tag[0] == 'd':
                        ins.then_inc(tag[1], 16)
                    else:
                        r = sched.rank[eng_name].get(tag[1])
                        if r is not None:
                            ins.then_inc(sched.esems[eng_name][r // EPOCH], 1)
            return body

        block.tensor(run("pe"))
        block.scalar(run("act"))
        block.vector(run("dve"))
        block.gpsimd(run("pool"))
        block.sync(run("sp"))


class Arena:
    def __init__(self, t, n):
        self.t = t
        self.n = n
        self.off = 0

    def alloc(self, name, shape, dt=F32):
        p = shape[0]
        n = int(np.prod(shape[1:]))
        if dt == F32:
            ap = self.t[0:p, self.off:self.off + n]
            self.off += n
        else:
            nf = (n + 1) // 2
            ap = self.t[0:p, self.off:self.off + nf].bitcast(BF16)[:, 0:n]
            self.off += nf
        self.off = (self.off + 7) // 8 * 8
        assert self.off <= self.n, ("arena overflow", name, self.off, self.n)
        if len(shape) == 3:
            ap = ap.rearrange("p (a b) -> p a b", a=shape[1])
        elif len(shape) == 4:
            ap = ap.rearrange("p (a b c) -> p a b c", a=shape[1], b=shape[2])
        return Buf(name, ap)


def build(debug=None):
    nc = bass.Bass("TRN2", target_bir_lowering=False)

    def din(name, shape):
        return nc.dram_tensor(name, list(shape), F32, kind="ExternalInput").ap()

    def dscr(name, shape, dt=BF16):
        return nc.dram_tensor(name, list(shape), dt, kind="Internal").ap()

    xT = din("xT", [2048, 4608])
    xtok = din("xtok", [4096, 2048])
    w_in = din("w_in", [2048, 5120])
    w_out = din("w_out", [2048, 2048])
    wq = din("wq", [2048, 2048])
    uT = din("uT", [2048, 16384])
    vtab = din("vtab", [16384, 2048])
    bcol_d = din("bcol", [128, 40])
    brow_in_d = din("brow_in", [1, 2048])
    brow_out_d = din("brow_out", [1, 2048])
    bs_row_d = din("bs_row", [1, 1024])
    vng_d = din("vng", [128, 1024])
    vnb_d = din("vnb", [128, 1024])
    ln1g_d = din("ln1g", [128, 2048])
    ln1b_d = din("ln1b", [128, 2048])
    ln2g_d = din("ln2g", [128, 2048])
    ln2b_d = din("ln2b", [128, 2048])
    wsT_d = din("wsT", [128, 1024])
    skT_d = din("skT", [128, 2048])
    bias_d = din("biasT", [5, 128, 10240])
    ident_d = din("ident", [128, 128])
    outp = nc.dram_tensor("out", [4096, 2048], F32, kind="ExternalOutput").ap()

    w_in_bf = dscr("w_in_bf", [2048, 5120])
    w_out_bf = dscr("w_out_bf", [2048, 2048])
    wq_bf = dscr("wq_bf", [2048, 2048])
    uT_bf = dscr("uT_bf", [2048, 16384])
    v_bf = dscr("v_bf", [16384, 2048])
    xT_bf = dscr("xT_bf", [2048, 4608])
    bias_bf = dscr("bias_bf", [5, 128, 10240])
    wsT_bf = dscr("wsT_bf", [128, 1024])
    skT_bf = dscr("skT_bf", [128, 2048])
    ident_bfd = dscr("ident_bfd", [128, 128])
    qT_s = dscr("qT_s", [1024, 4608])
    kT_s = dscr("kT_s", [1024, 4608])
    vaug_s = dscr("vaug_s", [4608, 1040])
    AT_s = dscr("AT_s", [1024, 4096])
    xn_s = dscr("xn_s", [4096, 2048], F32)
    xnT_s = dscr("xnT_s", [2048, 4096])

    stack = ExitStack()
    with stack:
        ARN = 52400
        at = stack.enter_context(nc.sbuf_tensor("arena", [128, ARN], F32))
        A = Arena(at, ARN)
        S = Sched(nc, stack)

        def ps(name, n, dt=F32):
            return Buf(name, stack.enter_context(nc.psum_tensor(name, [128, n], dt)))

        pA = [ps("pA0", 512), ps("pA1", 512)]
        pY = ps("pY", 2048)
        pS = ps("pS", 512)
        pT = ps("pT", 1024, BF16)

        ident_f = A.alloc("ident_f", [128, 128])
        ident_b = A.alloc("ident_b", [128, 128], BF16)
        ones_f = A.alloc("ones_f", [1, 128])
        bcol = A.alloc("bcol", [128, 40])
        bq8 = A.alloc("bq8", [128, 8])
        brow_in = A.alloc("brow_in", [1, 2048])
        brow_out = A.alloc("brow_out", [1, 2048])
        bs_row = A.alloc("bs_row", [1, 1024])
        small = A.alloc("small", [128, 64])
        persist_mark = A.off

        cst = Buf("cst", None)
        cst2 = Buf("cst2", None)

        def cast_dma(dst, src, trk=None):
            S.dma("pool", lambda e, dst=dst, src=src: e.dma_start(out=dst, in_=src), wr=(trk or cst))

        for r in range(16):
            cast_dma(w_in_bf[r * 128:(r + 1) * 128, :], w_in[r * 128:(r + 1) * 128, :])
        for r in range(16):
            cast_dma(xT_bf[r * 128:(r + 1) * 128, :], xT[r * 128:(r + 1) * 128, :])
        cast_dma(wsT_bf[:, :], wsT_d[:, :])
        cast_dma(skT_bf[:, :], skT_d[:, :])
        cast_dma(ident_bfd[:, :], ident_d[:, :])
        for c in range(5):
            cast_dma(bias_bf[c], bias_d[c], cst2)
        for r in range(4):
            cast_dma(w_out_bf[r * 512:(r + 1) * 512, :].rearrange("(a p) n -> p a n", p=128),
                     w_out[r * 512:(r + 1) * 512, :].rearrange("(a p) n -> p a n", p=128), cst2)
            cast_dma(wq_bf[r * 512:(r + 1) * 512, :].rearrange("(a p) n -> p a n", p=128),
                     wq[r * 512:(r + 1) * 512, :].rearrange("(a p) n -> p a n", p=128), cst2)
        for r in range(16):
            cast_dma(uT_bf[r * 128:(r + 1) * 128, :], uT[r * 128:(r + 1) * 128, :], cst2)
        for r in range(16):
            cast_dma(v_bf[r * 1024:(r + 1) * 1024, :].rearrange("(a p) n -> p a n", p=128),
                     vtab[r * 1024:(r + 1) * 1024, :].rearrange("(a p) n -> p a n", p=128), cst2)

        def ld(buf, dst, src, eng="sp", extra=()):
            S.dma(eng, lambda e, dst=dst, src=src: e.dma_start(out=dst, in_=src), wr=buf, extra=extra)

        def st(buf, dst, src, eng="sp"):
            S.dma(eng, lambda e, dst=dst, src=src: e.dma_start(out=dst, in_=src), rd=buf)

        ld(ident_f, ident_f[:, :], ident_d[:, :])
        ld(bcol, bcol[:, :], bcol_d[:, :])
        ld(brow_in, brow_in[:, :], brow_in_d[:, :])
        ld(brow_out, brow_out[:, :], brow_out_d[:, :])
        ld(bs_row, bs_row[:, :], bs_row_d[:, :])
        ld(ident_b, ident_b[:, :], ident_bfd[:, :], extra=[cst])
        S.op("dve", lambda e: e.memset(ones_f[:, :], 1.0), writes=[ones_f])
        S.op("dve", lambda e: e.tensor_scalar_mul(bq8[:, :], bcol[:, 16:24], 0.125),
             reads=[bcol], writes=[bq8])

        def mm(out, lhsT, rhs, start, stop, reads, writes):
            S.op("pe", lambda e, o=out, l=lhsT, r=rhs, s0=start, s1=stop:
                 e.matmul(o, lhsT=l, rhs=r, start=s0, stop=s1), reads=reads, writes=writes)

        def act(out, in_, func, reads, writes, bias=None, scale=None, ww_ok=False):
            kw = {}
            if bias is not None:
                kw["bias"] = bias
            if scale is not None:
                kw["scale"] = scale
            S.op("act", lambda e, o=out, i=in_, f=func, kw=kw: e.activation(o, i, f, **kw),
                 reads=reads, writes=writes, ww_ok=ww_ok)

        def dve(fn, reads, writes, eng="dve", ww_ok=False):
            S.op(eng, fn, reads=reads, writes=writes, ww_ok=ww_ok)

        def layernorm(src, nfree, g_bc, b_bc, dst, scr):
            nch = nfree // 512
            sAP, sB = src
            dAP, dB = dst
            for c in range(nch):
                dve(lambda e, c=c: e.bn_stats(scr[:, c * 6:(c + 1) * 6], sAP[:, c * 512:(c + 1) * 512]),
                    [sB, scr], [scr])
            dve(lambda e: e.bn_aggr(scr[:, 32:34], scr[:, 0:nch * 6]), [scr], [scr])
            dve(lambda e: e.tensor_scalar_add(scr[:, 34:35], scr[:, 33:34], LN_EPS), [scr], [scr])
            act(scr[:, 35:36], scr[:, 34:35], AF.Ln, [scr], [scr])
            act(scr[:, 36:37], scr[:, 35:36], AF.Exp, [scr], [scr], scale=-0.5)
            dve(lambda e: e.tensor_scalar(out=sAP, in0=sAP, scalar1=scr[:, 32:33], scalar2=scr[:, 36:37],
                                          op0=ALU.subtract, op1=ALU.mult), [sB, scr], [sB])
            gAP, gB = g_bc if isinstance(g_bc, tuple) else (g_bc[:, :], g_bc)
            bAP, bB = b_bc if isinstance(b_bc, tuple) else (b_bc[:, :], b_bc)
            gBl = gB if isinstance(gB, list) else [gB]
            bBl = bB if isinstance(bB, list) else [bB]
            dve(lambda e: e.tensor_tensor(out=sAP, in0=sAP, in1=gAP, op=ALU.mult), [sB] + gBl, [sB])
            dve(lambda e: e.tensor_tensor(out=dAP, in0=sAP, in1=bAP, op=ALU.add), [sB] + bBl, [dB])

        W = [A.alloc("W0", [128, 16, 512], BF16), A.alloc("W1", [128, 16, 512], BF16)]
        mark_w = A.off
        XB = [A.alloc("XB0", [128, 16, 512], BF16), A.alloc("XB1", [128, 16, 512], BF16)]
        stage_mark = A.off
        W1s = W + [A.alloc("W2", [128, 16, 512], BF16), A.alloc("W3", [128, 16, 512], BF16)]
        uTb = A.alloc("uTb", [128, 8, 512])
        vg = A.alloc("vg", [128, 4, 1024])
        vng = A.alloc("vng", [128, 1024])
        vnb = A.alloc("vnb", [128, 1024])
        vn = A.alloc("vn", [128, 1024], BF16)
        vaug = [A.alloc("vaug%d" % i, [128, 16, 65], BF16) for i in range(4)]
        qk = [A.alloc("qk0", [128, 512], BF16), A.alloc("qk1", [128, 512], BF16)]
        ATb = A.alloc("ATb", [128, 8, 128], BF16)
        wsT = A.alloc("wsT", [128, 8, 128], BF16)
        ld(vng, vng[:, :], vng_d[:, :])
        ld(vnb, vnb[:, :], vnb_d[:, :])
        ld(wsT, wsT[:, :, :], wsT_bf.rearrange("p (g q) -> p g q", g=8), extra=[cst])
        for i in range(4):
            dve(lambda e, i=i: e.memset(vaug[i][:, :, 64:65], 1.0), [], [vaug[i]])

        w_in_v = w_in_bf.rearrange("(kc p) n -> p kc n", p=128)
        xT_v = xT_bf.rearrange("(kc p) n -> p kc n", p=128)
        qT_v = qT_s.rearrange("(j p) n -> p j n", p=128)
        kT_v = kT_s.rearrange("(j p) n -> p j n", p=128)
        AT_v = AT_s.rearrange("(g p) n -> p g n", p=128)
        wcount = 0
        pacount = 0
        qkcount = 0
        NBLK1 = 9
        for blk in range(NBLK1):
            xb = XB[blk % 2]
            for kq in range(4):
                ld(xb, xb[:, kq * 4:(kq + 1) * 4, :], xT_v[:, kq * 4:(kq + 1) * 4, blk * 512:(blk + 1) * 512],
                   extra=[cst])
            for cb in range(10):
                wb = W1s[wcount % 4]
                wcount += 1
                for kq in range(4):
                    ld(wb, wb[:, kq * 4:(kq + 1) * 4, :], w_in_v[:, kq * 4:(kq + 1) * 4, cb * 512:(cb + 1) * 512],
                       extra=[cst])
                if cb in (0, 1, 4, 5, 6, 7):
                    for g in range(4):
                        pa = pA[pacount % 2]
                        pacount += 1
                        for kc in range(16):
                            mm(pa[:, :], wb[:, kc, g * 128:(g + 1) * 128], xb[:, kc, :], kc == 0, kc == 15,
                               [wb, xb], [pa])
                        if cb < 2:
                            gi = cb * 4 + g
                            act(uTb[:, gi, :], pa[:, :], AF.Gelu, [pa, bcol], [uTb], bias=bcol[:, gi:gi + 1])
                        else:
                            j = (cb - 4) * 4 + g if cb < 6 else (cb - 6) * 4 + g
                            ob = qk[qkcount % 2]
                            qkcount += 1
                            if cb < 6:
                                act(ob[:, :], pa[:, :], AF.Identity, [pa, bq8], [ob], bias=bq8[:, j:j + 1], scale=0.125)
                                st(ob, qT_v[:, j, blk * 512:(blk + 1) * 512], ob[:, :])
                            else:
                                act(ob[:, :], pa[:, :], AF.Identity, [pa, bcol], [ob], bias=bcol[:, 24 + j:25 + j])
                                st(ob, kT_v[:, j, blk * 512:(blk + 1) * 512], ob[:, :])
                else:
                    for tt in range(4):
                        pa = pA[pacount % 2]
                        pacount += 1
                        boff = (cb - 2) * 512 if cb < 4 else 1024 + (cb - 8) * 512
                        mm(pa[:, :], ones_f[0:1, :], brow_in[0:1, boff:boff + 512], True, False,
                           [ones_f, brow_in], [pa])
                        for kc in range(16):
                            mm(pa[:, :], xb[:, kc, tt * 128:(tt + 1) * 128], wb[:, kc, :], False, kc == 15,
                               [wb, xb], [pa])
                        if cb < 4:
                            act(vg[:, tt, (cb - 2) * 512:(cb - 1) * 512], pa[:, :], AF.Gelu, [pa], [vg])
                        else:
                            h0 = (cb - 8) * 8
                            act(vaug[tt][:, h0:h0 + 8, 0:64], pa[:, :].rearrange("p (h d) -> p h d", h=8),
                                AF.Copy, [pa], [vaug[tt]])
            for tt in range(4):
                et = blk * 4 + tt
                st(vaug[tt], vaug_s[et * 128:(et + 1) * 128, :], vaug[tt][:, :, :].rearrange("p h d -> p (h d)"))
                if et < 2 or et >= 34:
                    continue
                t = et - 2
                vgt = vg[:, tt, :]
                layernorm((vgt, vg), 1024, vng, vnb, (vn[:, :], vn), small)
                for g in range(8):
                    mm(pY[:, g * 128:(g + 1) * 128], ones_f[0:1, :], bs_row[0:1, g * 128:(g + 1) * 128], True, False,
                       [ones_f, bs_row], [pY])
                    mm(pY[:, g * 128:(g + 1) * 128], vn[:, g * 128:(g + 1) * 128], wsT[:, g, :], False, True,
                       [vn, wsT], [pY])
                dve(lambda e, tt=tt: e.tensor_tensor(out=ATb[:, :, :],
                                                     in0=pY[:, 0:1024].rearrange("p (g q) -> p g q", g=8),
                                                     in1=uTb[:, :, tt * 128:(tt + 1) * 128], op=ALU.mult),
                    [pY, uTb], [ATb])
                st(ATb, AT_v[:, :, t * 128:(t + 1) * 128], ATb[:, :, :])

        S.barrier()
        A.off = persist_mark
        WO = A.alloc("WO", [128, 16, 2048], BF16)
        biasT = A.alloc("biasT", [128, 16, 5, 128], BF16)
        kTw = A.alloc("kTw", [128, 8, 640], BF16)
        vaw = A.alloc("vaw", [128, 5, 1040], BF16)
        qTt = A.alloc("qTt", [128, 8, 128], BF16)
        PT = A.alloc("PT", [128, 640], BF16)
        PT2 = A.alloc("PT2", [128, 640], BF16)
        Bb = A.alloc("Bb", [128, 1024], BF16)
        BT = A.alloc("BT", [128, 8, 128], BF16)
        ATt = A.alloc("ATt", [128, 8, 128], BF16)
        xnb = A.alloc("xnb", [128, 2048], BF16)
        xnT = A.alloc("xnT", [128, 16, 128], BF16)
        xt = A.alloc("xt", [128, 2048])
        rr = A.alloc("rr", [128, 2048])
        ln1g = A.alloc("ln1g", [128, 2048])
        ln1b = A.alloc("ln1b", [128, 2048])
        rec = A.alloc("rec", [128, 8])
        ld(ln1g, ln1g[:, :], ln1g_d[:, :])
        ld(ln1b, ln1b[:, :], ln1b_d[:, :])
        wo_v = w_out_bf.rearrange("(kc p) n -> p kc n", p=128)
        for kq in range(8):
            ld(WO, WO[:, kq * 2:(kq + 1) * 2, :], wo_v[:, kq * 2:(kq + 1) * 2, :])
        vaug_v = vaug_s.rearrange("(e p) f -> p e f", p=128)
        xnT_v = xnT_s.rearrange("(kc p) n -> p kc n", p=128)
        loaded_cls = -1
        NT = 32
        for t in range(NT):
            cls = 0 if t == 0 else 1 if t == 1 else 3 if t == 30 else 4 if t == 31 else 2
            if cls != loaded_cls:
                bv = bias_bf[cls].rearrange("p (h c q) -> p h c q", h=16, c=5)
                for hq in range(4):
                    ld(biasT, biasT[:, hq * 4:(hq + 1) * 4, :, :], bv[:, hq * 4:(hq + 1) * 4, :, :])
                loaded_cls = cls
            ld(qTt, qTt[:, :, :], qT_v[:, :, (t + 2) * 128:(t + 3) * 128])
            ld(kTw, kTw[:, :, :], kT_v[:, :, t * 128:t * 128 + 640])
            ld(vaw, vaw[:, :, :], vaug_v[:, t:t + 5, :])
            ld(ATt, ATt[:, :, :], AT_v[:, :, t * 128:(t + 1) * 128])
            ld(xt, xt[:, :], xtok[t * 128:(t + 1) * 128, :])
            def na_S(h):
                j, base = h // 2, 64 * (h % 2)
                for ck in range(5):
                    if h % 2 == 0:
                        dst, dB = (pS[:, ck * 128:(ck + 1) * 128], pS) if ck < 4 else (pA[0][:, 0:128], pA[0])
                    else:
                        dst, dB = pY[:, ck * 128:(ck + 1) * 128], pY
                    mm(dst, kTw[base:base + 64, j, ck * 128:(ck + 1) * 128], qTt[base:base + 64, j, :], True, False,
                       [kTw, qTt], [dB])
                    mm(dst, ident_b[:, :], biasT[:, h, ck, :], False, True, [ident_b, biasT], [dB])

            def na_exp(h):
                if h % 2 == 0:
                    act(PT[:, 0:512], pS[:, :], AF.Exp, [pS], [PT])
                    act(PT[:, 512:640], pA[0][:, 0:128], AF.Exp, [pA[0]], [PT])
                else:
                    act(PT2[:, 0:640], pY[:, 0:640], AF.Exp, [pY], [PT2])

            def na_PV(h):
                pt_ = PT if h % 2 == 0 else PT2
                ob, oB = (pA[1][:, 0:65], pA[1]) if h % 2 == 0 else (pY[:, 1024:1089], pY)
                for ck in range(5):
                    mm(ob, pt_[:, ck * 128:(ck + 1) * 128], vaw[:, ck, h * 65:(h + 1) * 65], ck == 0, ck == 4,
                       [pt_, vaw], [oB])
                rc = rec[:, (h % 2):(h % 2) + 1]
                dve(lambda e, ob=ob, rc=rc: e.reciprocal(rc, ob[:, 64:65]), [oB], [rec])
                dve(lambda e, h=h, ob=ob, rc=rc: e.tensor_scalar_mul(Bb[:, h * 64:(h + 1) * 64], ob[:, 0:64], rc),
                    [oB, rec], [Bb], ww_ok=True)

            na_S(0)
            na_exp(0)
            for h in range(16):
                if h + 1 < 16:
                    na_S(h + 1)
                    na_exp(h + 1)
                na_PV(h)
            for j in range(8):
                S.op("pe", lambda e, j=j: e.transpose(pT[:, j * 128:(j + 1) * 128], Bb[:, j * 128:(j + 1) * 128],
                                                      ident_b[:, :]), reads=[Bb, ident_b], writes=[pT])
            dve(lambda e: e.tensor_copy(BT[:, :, :], pT[:, :].rearrange("p (j q) -> p j q", j=8)), [pT], [BT])
            for nb in range(4):
                dst = pY[:, nb * 512:(nb + 1) * 512]
                mm(dst, ones_f[0:1, :], brow_out[0:1, nb * 512:(nb + 1) * 512], True, False, [ones_f, brow_out], [pY])
                for g in range(8):
                    mm(dst, ATt[:, g, :], WO[:, g, nb * 512:(nb + 1) * 512], False, False, [ATt, WO], [pY])
                for j in range(8):
                    mm(dst, BT[:, j, :], WO[:, 8 + j, nb * 512:(nb + 1) * 512], False, j == 7, [BT, WO], [pY])
            dve(lambda e: e.scalar_tensor_tensor(out=rr[:, :], in0=xt[:, :], scalar=ALPHA, in1=pY[:, :],
                                                 op0=ALU.mult, op1=ALU.add), [xt, pY], [rr])
            layernorm((rr[:, :], rr), 2048, ln1g, ln1b, (rr[:, :], rr), small)
            st(rr, xn_s[t * 128:(t + 1) * 128, :], rr[:, :])
            act(xnb[:, :], rr[:, :], AF.Copy, [rr], [xnb])
            for half in range(2):
                for j in range(8):
                    kc = half * 8 + j
                    S.op("pe", lambda e, j=j, kc=kc: e.transpose(pT[:, j * 128:(j + 1) * 128],
                                                                 xnb[:, kc * 128:(kc + 1) * 128], ident_b[:, :]),
                         reads=[xnb, ident_b], writes=[pT])
                dve(lambda e, half=half: e.tensor_copy(xnT[:, half * 8:(half + 1) * 8, :],
                                                       pT[:, :].rearrange("p (j q) -> p j q", j=8)), [pT], [xnT])
            st(xnT, xnT_v[:, :, t * 128:(t + 1) * 128], xnT[:, :, :])

        S.barrier()
        A.off = mark_w
        TB = 256
        NTT = TB // 128
        XB3 = [A.alloc("XB3_0", [128, 16, TB], BF16), A.alloc("XB3_1", [128, 16, TB], BF16)]
        V = [A.alloc("V0", [128, 4, 2048], BF16), A.alloc("V1", [128, 4, 2048], BF16)]
        qpT = V[0]
        qpT_ap = V[0][:, :, :].rearrange("p a b -> p (a b)").rearrange("p (c t) -> p c t", c=16)
        skT = A.alloc("skT", [128, 16, 128], BF16)
        mg_off = A.off
        MG = [A.alloc("MGa", [128, 4, 512], BF16), A.alloc("MGb", [128, 4, 512], BF16)]
        mk_off = A.off
        MK = [A.alloc("MKa", [128, 4, 512], BF16), A.alloc("MKb", [128, 4, 512], BF16)]
        assert mk_off - mg_off == 2048 and A.off - mk_off == 2048
        ln2g_ap = at[:, mg_off:mg_off + 2048]
        ln2b_ap = at[:, mk_off:mk_off + 2048]
        PB = [A.alloc("PBa", [128, 4, 512], BF16)]
        PG1 = A.alloc("PG1", [128, 4, 4, 128])
        r_sb = A.alloc("r_sb", [128, NTT, 16, 128])
        Wt = A.alloc("Wt", [128, 512], BF16)
        WTb = [A.alloc("WTb0", [128, 512], BF16), A.alloc("WTb1", [128, 512], BF16)]
        gH = [A.alloc("gH0", [128, 512]), A.alloc("gH1", [128, 512])]
        s_sb = A.alloc("s_sb", [128, NTT, 16, 128])
        yacc = A.alloc("yacc", [128, NTT, 2048])
        xres = A.alloc("xres", [128, 2048])
        top = A.alloc("top", [128, 16, 16])
        cand = A.alloc("cand", [128, 4, 256])
        c16 = A.alloc("c16", [128, 8, 16])
        e16 = A.alloc("e16", [128, 8, 16])
        tmp = A.alloc("tmp", [128, 256])
        sm3 = A.alloc("sm3", [128, 64])
        biasg = A.alloc("biasg", [128, NTT, 8])
        cth = A.alloc("cth", [128, NTT, 8])
        ld(skT, skT[:, :, :], skT_bf.rearrange("p (c n) -> p c n", c=16))
        print("stage3 arena", A.off, A.n)
        wq_v = wq_bf.rearrange("(kc p) n -> p kc n", p=128)
        uT_v = uT_bf.rearrange("(kc p) n -> p kc n", p=128)
        NEB = 32
        NIT = NEB * NTT
        for tb in range(4096 // TB):
            xb = XB3[tb % 2]
            for kq in range(4):
                ld(xb, xb[:, kq * 4:(kq + 1) * 4, :], xnT_v[:, kq * 4:(kq + 1) * 4, tb * TB:(tb + 1) * TB])
            for cbq in range(4):
                wb = W[wcount % 2]
                wcount += 1
                for kq in range(4):
                    ld(wb, wb[:, kq * 4:(kq + 1) * 4, :], wq_v[:, kq * 4:(kq + 1) * 4, cbq * 512:(cbq + 1) * 512])
                for g in range(4):
                    c = cbq * 4 + g
                    pa = pA[pacount % 2]
                    pacount += 1
                    for kc in range(16):
                        mm(pa[:, 0:TB], wb[:, kc, g * 128:(g + 1) * 128], xb[:, kc, :], kc == 0, kc == 15,
                           [wb, xb], [pa])
                    act(qpT_ap[:, c, 0:TB], pa[:, 0:TB], AF.Copy, [pa], [qpT])
            for tt in range(NTT):
                for cg in range(4):
                    pa = pA[pacount % 2]
                    pacount += 1
                    for ci in range(4):
                        c = cg * 4 + ci
                        mm(pa[:, ci * 128:(ci + 1) * 128], qpT_ap[:, c, tt * 128:(tt + 1) * 128], skT[:, c, :],
                           True, True, [qpT, skT], [pa])
                    dve(lambda e, tt=tt, cg=cg, pa=pa: e.tensor_copy(
                        s_sb[:, tt, cg * 4:(cg + 1) * 4, :], pa[:, :].rearrange("p (c n) -> p c n", c=4)),
                        [pa], [s_sb])
                for c in range(16):
                    dve(lambda e, tt=tt, c=c: e.max(out=top[:, c, 0:8], in_=s_sb[:, tt, c, :]), [s_sb], [top])
                    dve(lambda e, tt=tt, c=c: e.match_replace(out=tmp[:, 0:128], in_to_replace=top[:, c, 0:8],
                                                              in_values=s_sb[:, tt, c, :], imm_value=-1e30),
                        [s_sb, top], [tmp])
                    dve(lambda e, c=c: e.max(out=top[:, c, 8:16], in_=tmp[:, 0:128]), [tmp], [top])
                topv = top[:, :, :].rearrange("p (h two) k -> p h two k", two=2)
                for hh in range(2):
                    hs = slice(hh * 4, hh * 4 + 4)
                    dve(lambda e, hs=hs: e.tensor_tensor(
                        out=cand[:, :, :].rearrange("p h (a b) -> p h a b", a=16),
                        in0=topv[:, hs, 0, :].unsqueeze(3).to_broadcast([128, 4, 16, 16]),
                        in1=topv[:, hs, 1, :].unsqueeze(2).to_broadcast([128, 4, 16, 16]), op=ALU.add),
                        [top], [cand])
                    for h4 in range(4):
                        h = hh * 4 + h4
                        dve(lambda e, h=h, h4=h4: e.max(out=c16[:, h, 0:8], in_=cand[:, h4, :]), [cand], [c16])
                        dve(lambda e, h=h, h4=h4: e.match_replace(out=tmp[:, :], in_to_replace=c16[:, h, 0:8],
                                                                  in_values=cand[:, h4, :], imm_value=-1e30),
                            [cand, c16], [tmp])
                        dve(lambda e, h=h: e.max(out=c16[:, h, 8:16], in_=tmp[:, :]), [tmp], [c16])
                dve(lambda e: e.tensor_scalar_add(sm3[:, 0:8], c16[:, :, 15], -TAU_EPS), [c16], [sm3])
                dve(lambda e: e.tensor_tensor(out=e16[:, :, :], in0=c16[:, :, :],
                                              in1=c16[:, :, 0:1].to_broadcast([128, 8, 16]), op=ALU.subtract),
                    [c16], [e16])
                act(e16[:, :, :], e16[:, :, :], AF.Exp, [e16], [e16])
                dve(lambda e: e.reduce_sum(out=sm3[:, 8:16], in_=e16[:, :, :], axis=AX.X), [e16, sm3], [sm3])
                act(sm3[:, 16:24], sm3[:, 8:16], AF.Ln, [sm3], [sm3])
                dve(lambda e: e.tensor_tensor(out=sm3[:, 24:32], in0=sm3[:, 0:8], in1=c16[:, :, 0], op=ALU.subtract),
                    [sm3, c16], [sm3])
                dve(lambda e, tt=tt: e.tensor_tensor(out=biasg[:, tt, :], in0=sm3[:, 24:32], in1=sm3[:, 16:24],
                                                     op=ALU.subtract), [sm3], [biasg])
                act(cth[:, tt, :], biasg[:, tt, :], AF.Exp, [biasg], [cth])
                dve(lambda e, tt=tt: e.tensor_scalar_mul(cth[:, tt, :], cth[:, tt, :], -1.0), [cth], [cth])
                dve(lambda e, tt=tt, topv=topv: e.tensor_tensor(out=sm3[:, 32:40], in0=biasg[:, tt, :],
                                                                in1=topv[:, :, 1, 0], op=ALU.add),
                    [biasg, top, sm3], [sm3])
                dve(lambda e: e.tensor_tensor(out=sm3[:, 32:40], in0=sm3[:, 32:40], in1=sm3[:, 0:8],
                                              op=ALU.subtract), [sm3], [sm3])
                sv = s_sb[:, tt, :, :].rearrange("p (h two) n -> p h two n", two=2)
                rv = r_sb[:, tt, :, :].rearrange("p (h two) n -> p h two n", two=2)
                dve(lambda e, sv=sv, rv=rv: e.tensor_tensor(
                    out=rv[:, :, 0, :], in0=sm3[:, 0:8].unsqueeze(2).to_broadcast([128, 8, 128]),
                    in1=sv[:, :, 0, :], op=ALU.subtract), [s_sb, sm3], [r_sb])
                dve(lambda e, sv=sv, rv=rv: e.tensor_copy(rv[:, :, 1, :], sv[:, :, 1, :]), [s_sb], [r_sb])
                dve(lambda e, sv=sv: e.tensor_tensor(out=sv[:, :, 0, :], in0=sv[:, :, 0, :],
                                                     in1=sm3[:, 32:40].unsqueeze(2).to_broadcast([128, 8, 128]),
                                                     op=ALU.add), [s_sb, sm3], [s_sb])
                dve(lambda e, sv=sv, topv=topv: e.tensor_tensor(
                    out=sv[:, :, 1, :], in0=sv[:, :, 1, :],
                    in1=topv[:, :, 1, 0:1].to_broadcast([128, 8, 128]), op=ALU.subtract),
                    [s_sb, top], [s_sb])
                act(s_sb[:, tt, :, :], s_sb[:, tt, :, :], AF.Exp, [s_sb], [s_sb])

            wbase = wcount
            wcount += NEB

            def load_W(eb):
                wb = W[(wbase + eb) % 2]
                for kq in range(4):
                    ld(wb, wb[:, kq * 4:(kq + 1) * 4, :], uT_v[:, kq * 4:(kq + 1) * 4, eb * 512:(eb + 1) * 512])

            def load_V(eb):
                vb = V[eb % 2]
                vsrc = v_bf[eb * 512:(eb + 1) * 512, :].rearrange("(c p) n -> p c n", p=128)
                for c in range(4):
                    ld(vb, vb[:, c, :], vsrc[:, c, :])

            def st_P(it):
                eb, tt = divmod(it, NTT)
                i0 = eb * 4
                rv = r_sb[:, tt, :, :].rearrange("p (h two) n -> p h two n", two=2)
                sv = s_sb[:, tt, :, :].rearrange("p (h two) n -> p h two n", two=2)
                dve(lambda e, rv=rv, i0=i0: e.tensor_tensor(
                    out=MK[0][:, :, :].rearrange("p h (a j) -> p h a j", a=4),
                    in0=rv[:, 0:4, 1, :].unsqueeze(2).to_broadcast([128, 4, 4, 128]),
                    in1=rv[:, 0:4, 0, i0:i0 + 4].unsqueeze(3).to_broadcast([128, 4, 4, 128]),
                    op=ALU.subtract), [r_sb], [MK[0]], eng="pool")
                dve(lambda e, sv=sv, i0=i0: e.tensor_tensor(
                    out=PG1[:, :, :, :],
                    in0=sv[:, 4:8, 0, i0:i0 + 4].unsqueeze(3).to_broadcast([128, 4, 4, 128]),
                    in1=sv[:, 4:8, 1, :].unsqueeze(2).to_broadcast([128, 4, 4, 128]),
                    op=ALU.mult), [s_sb], [PG1], eng="pool")

            def st_PB(it):
                eb, tt = divmod(it, NTT)
                i0 = eb * 4
                sv = s_sb[:, tt, :, :].rearrange("p (h two) n -> p h two n", two=2)
                for hh in range(4):
                    for a in range(4):
                        act(PB[0][:, hh, a * 128:(a + 1) * 128], sv[:, hh, 1, :], AF.Copy, [s_sb], [PB[0]],
                            scale=sv[:, hh, 0, i0 + a:i0 + a + 1], ww_ok=True)
                for hh in range(4):
                    h = 4 + hh
                    pv = PG1[:, hh, :, :].rearrange("p a j -> p (a j)")
                    act(MK[1][:, hh, :], pv, AF.Sign, [PG1, cth], [MK[1]], bias=cth[:, tt, h:h + 1], ww_ok=True)

            def st_M(it):
                for hh in range(4):
                    dve(lambda e, hh=hh: e.scalar_tensor_tensor(
                        out=MG[0][:, hh, :], in0=MK[0][:, hh, :], scalar=0.0, in1=PB[0][:, hh, :],
                        op0=ALU.is_ge, op1=ALU.mult), [MK[0], PB[0]], [MG[0]], ww_ok=True)
                for hh in range(4):
                    pv = PG1[:, hh, :, :].rearrange("p a j -> p (a j)")
                    dve(lambda e, hh=hh, pv=pv: e.scalar_tensor_tensor(
                        out=MG[1][:, hh, :], in0=MK[1][:, hh, :], scalar=0.0, in1=pv,
                        op0=ALU.max, op1=ALU.mult), [MK[1], PG1], [MG[1]], ww_ok=True)

            def st_Gs(it):
                for h in range(8):
                    mm(pS[:, :], ident_b[:, :], MG[h // 4][:, h % 4, :], h == 0, h == 7, [ident_b, MG[h // 4]], [pS])

            def st_H(it):
                eb, tt = divmod(it, NTT)
                wb = W[(wbase + eb) % 2]
                pa = pA[it % 2]
                for kc in range(16):
                    mm(pa[:, :], xb[:, kc, tt * 128:(tt + 1) * 128], wb[:, kc, :], kc == 0, kc == 15, [wb, xb], [pa])

            def st_gelu(it):
                act(gH[it % 2][:, :], pA[it % 2][:, :], AF.Gelu, [pA[it % 2]], [gH[it % 2]])

            def st_W(it):
                g_ = gH[it % 2]
                dve(lambda e, g_=g_: e.tensor_tensor(out=Wt[:, :], in0=g_[:, :], in1=pS[:, :], op=ALU.mult),
                    [g_, pS], [Wt])

            def st_T(it):
                for c in range(4):
                    S.op("pe", lambda e, c=c: e.transpose(pT[:, c * 128:(c + 1) * 128],
                                                          Wt[:, c * 128:(c + 1) * 128], ident_b[:, :]),
                         reads=[Wt, ident_b], writes=[pT])

            def st_WTc(it):
                act(WTb[it % 2][:, :], pT[:, 0:512], AF.Copy, [pT], [WTb[it % 2]])

            def st_y(it):
                eb, tt = divmod(it, NTT)
                vb = V[eb % 2]
                wt_ = WTb[it % 2]
                for nb in range(4):
                    for c in range(4):
                        mm(pY[:, nb * 512:(nb + 1) * 512], wt_[:, c * 128:(c + 1) * 128],
                           vb[:, c, nb * 512:(nb + 1) * 512], c == 0, c == 3, [wt_, vb], [pY])

            def st_yacc(it):
                eb, tt = divmod(it, NTT)
                if eb == 0:
                    dve(lambda e, tt=tt: e.tensor_copy(yacc[:, tt, :], pY[:, :]), [pY], [yacc])
                else:
                    dve(lambda e, tt=tt: e.tensor_tensor(out=yacc[:, tt, :], in0=yacc[:, tt, :], in1=pY[:, :],
                                                         op=ALU.add), [pY, yacc], [yacc])

            load_W(0)
            load_W(1)
            load_V(0)
            load_V(1)
            st_P(0)
            st_PB(0)
            st_M(0)
            st_H(0)
            st_gelu(0)
            for r in range(NIT + 2):
                if r < NIT:
                    st_Gs(r)
                if r + 1 < NIT:
                    if (r + 1) % NTT == 0:
                        ebn = (r + 1) // NTT + 1
                        if ebn < NEB:
                            load_W(ebn)
                    st_P(r + 1)
                    st_PB(r + 1)
                    st_H(r + 1)
                if r >= 2:
                    st_yacc(r - 2)
                if r < NIT:
                    st_W(r)
                    st_T(r)
                if r + 1 < NIT:
                    st_M(r + 1)
                    st_gelu(r + 1)
                if r < NIT:
                    st_WTc(r)
                if 1 <= r <= NIT:
                    st_y(r - 1)
                if r >= NTT and r % NTT == 0 and r // NTT + 1 < NEB:
                    load_V(r // NTT + 1)
            ld(MG[0], ln2g_ap[:, 0:1024], ln2g_d[:, 0:1024])
            ld(MG[1], ln2g_ap[:, 1024:2048], ln2g_d[:, 1024:2048])
            ld(MK[0], ln2b_ap[:, 0:1024], ln2b_d[:, 0:1024])
            ld(MK[1], ln2b_ap[:, 1024:2048], ln2b_d[:, 1024:2048])
            for tt in range(NTT):
                t = tb * NTT + tt
                ld(xres, xres[:, :], xn_s[t * 128:(t + 1) * 128, :])
                dve(lambda e, tt=tt: e.scalar_tensor_tensor(out=xres[:, :], in0=xres[:, :], scalar=ALPHA,
                                                            in1=yacc[:, tt, :], op0=ALU.mult, op1=ALU.add),
                    [xres, yacc], [xres])
                layernorm((xres[:, :], xres), 2048, (ln2g_ap, [MG[0], MG[1]]), (ln2b_ap, [MK[0], MK[1]]), (xres[:, :], xres), sm3)
                st(xres, outp[t * 128:(t + 1) * 128, :], xres[:, :])
            S.barrier()

        S.barrier()
        S.finalize()
        S.check()
        with nc.Block() as block:
            S.emit(block)
    return nc


def _bias_tables(rpb, q):
    out = np.full((5, 128, 16, 5, 128), NEG, np.float32)
    qc = np.arange(64)
    cs = np.clip(qc - 8, 0, 48)
    kc = np.arange(64)
    colok = (kc[:, None] >= cs[None, :]) & (kc[:, None] < cs[None, :] + 16)
    coff = kc[:, None] - qc[None, :] + 15
    for ci, t in enumerate((0, 1, 2, 30, 31)):
        for qp in range(2):
            R = 64 * q + 2 * t + qp
            rs = min(max(R - 4, 0), 248)
            seen = set()
            order = list(range(5))
            slots = []
            for ck in order:
                for kp in range(2):
                    lr = 2 * t - 4 + 2 * ck + kp
                    wrapped = (q == 0 and lr < 0) or (q == 3 and lr >= 64)
                    KR = 64 * q + lr + (8 if (q == 0 and lr < 0) else 0) - (8 if (q == 3 and lr >= 64) else 0)
                    slots.append((wrapped, ck, kp, KR))
            slots.sort(key=lambda s: s[0])
            for wrapped, ck, kp, KR in slots:
                if KR < rs or KR >= rs + 8 or KR in seen:
                    continue
                seen.add(KR)
                ro = KR - R + 7
                vals = rpb[:, ro, :][:, np.clip(coff, 0, 30)]
                vals = np.where(colok[None], vals, NEG)
                out[ci, kp * 64:(kp + 1) * 64, :, ck, qp * 64:(qp + 1) * 64] = vals.transpose(1, 0, 2)
            assert len(seen) == 8
    return out.reshape(5, 128, 16 * 5 * 128)


_NC_CACHE = {}


def kernel(x, w_in, b_in, v_norm_g, v_norm_b, w_spatial, b_spatial, rpb, w_out, b_out,
           ln1_g, ln1_b, peer_wq, peer_subkeys, peer_u, peer_v, ln2_g, ln2_b):
    f = lambda a: np.ascontiguousarray(np.asarray(a, dtype=np.float32))
    x = f(x)
    bc = lambda v, n: f(np.broadcast_to(np.asarray(v, np.float32).reshape(1, n), (128, n)))
    b_in0 = np.asarray(b_in, np.float32)[0]
    shared = {
        "w_in": f(w_in[0]), "w_out": f(w_out[0]), "wq": f(peer_wq[0]),
        "uT": f(np.asarray(peer_u[0]).T), "vtab": f(peer_v[0]),
        "bcol": f(b_in0.reshape(40, 128).T),
        "brow_in": f(np.concatenate([b_in0[1024:2048], b_in0[4096:5120]]).reshape(1, 2048)),
        "brow_out": f(np.asarray(b_out[0]).reshape(1, 2048)),
        "bs_row": f(np.asarray(b_spatial[0]).reshape(1, 1024)),
        "vng": bc(v_norm_g[0], 1024), "vnb": bc(v_norm_b[0], 1024),
        "ln1g": bc(ln1_g[0], 2048), "ln1b": bc(ln1_b[0], 2048),
        "ln2g": bc(ln2_g[0], 2048), "ln2b": bc(ln2_b[0], 2048),
        "wsT": f(np.asarray(w_spatial[0]).transpose(2, 0, 1).reshape(128, 1024)),
        "skT": f(np.asarray(peer_subkeys[0]).transpose(3, 0, 1, 2).reshape(128, 2048)),
        "ident": np.eye(128, dtype=np.float32),
    }
    rpb0 = np.asarray(rpb[0], np.float32)
    tabs = [_bias_tables(rpb0, q) for q in range(4)]
    in_maps = []
    for c in range(8):
        b, q = c // 4, c % 4
        xb = x[b]
        rows = []
        for lr in range(-4, 68):
            gr = 64 * q + lr
            if q == 0 and lr < 0:
                gr = lr + 8
            if q == 3 and lr >= 64:
                gr = 64 * q + lr - 8
            rows.append(gr)
        rows = np.asarray(rows)
        tok = (rows[:, None] * 64 + np.arange(64)[None, :]).reshape(-1)
        xext = xb[tok]
        m = dict(shared)
        m["xT"] = np.ascontiguousarray(xext.T)
        m["xtok"] = np.ascontiguousarray(xb[4096 * q:4096 * (q + 1)])
        m["biasT"] = tabs[q]
        in_maps.append(m)
    if "nc" not in _NC_CACHE:
        _NC_CACHE["nc"] = build()
    res = run_bass_kernel_spmd(_NC_CACHE["nc"], in_maps, core_ids=list(range(8)))
    out = np.empty((2, 16384, 2048), np.float32)
    for c in range(8):
        b, q = c // 4, c % 4
        out[b, 4096 * q:4096 * (q + 1)] = res.results[c]["out"]
    return out
```

```python
import numpy as np
from contextlib import ExitStack
import concourse.bass as bass
import concourse.mybir as mybir
from concourse.bass_utils import run_bass_kernel_spmd

F32, BF16 = mybir.dt.float32, mybir.dt.bfloat16
AF = mybir.ActivationFunctionType
ALU = mybir.AluOpType
AX = mybir.AxisListType

ALPHA = 2.0 ** 0.25
LN_EPS = 1e-5
NEG = -30000.0
TAU_EPS = 2e-4
ENGS = ("pe", "act", "dve", "pool", "sp")
EPOCH = 30000


class Buf:
    def __init__(self, name, t):
        self.name = name
        self.t = t
        self.last_w = None
        self.reads = []
        self.dsem = None
        self.dcount = 0

    def __getitem__(self, k):
        return self.t[k]


class Sched:
    def __init__(self, nc, stack):
        self.nc = nc
        self.stack = stack
        self.ops = {e: [] for e in ENGS}
        self.ecount = {e: 0 for e in ENGS}
        self.dbufs = []

    def _dsem(self, b):
        if b.dsem is None:
            b.dsem = self.stack.enter_context(self.nc.semaphore("d_" + b.name))
            self.dbufs.append(b)
        return b.dsem

    def _deps(self, reads, writes, ww_eng=None):
        deps = []
        for b in reads:
            if b.last_w is not None:
                deps.append(b.last_w)
        for b in writes:
            if b.last_w is not None:
                lw = b.last_w
                if not (ww_eng is not None and lw[0] == 'e' and lw[1] == ww_eng):
                    deps.append(lw)
            deps.extend(b.reads)
        return deps

    def op(self, eng, fn, reads=(), writes=(), ww_ok=False):
        deps = self._deps(reads, writes, eng if ww_ok else None)
        idx = self.ecount[eng]
        self.ecount[eng] += 1
        me = ('e', eng, idx)
        for b in reads:
            b.reads.append(me)
        for b in writes:
            b.last_w = me
            b.reads = []
        self.ops[eng].append([deps, fn, ('e', idx)])

    def dma(self, eng, fn, rd=None, wr=None, extra=()):
        b = rd if rd is not None else wr
        deps = self._deps([rd] if rd is not None else [], [wr] if wr is not None else [])
        for x in extra:
            if x.last_w is not None:
                deps.append(x.last_w)
        sem = self._dsem(b)
        b.dcount += 16
        me = ('d', b, b.dcount)
        if rd is not None:
            b.reads.append(me)
        else:
            b.last_w = me
            b.reads = []
        self.ops[eng].append([deps, fn, ('d', sem)])

    def barrier(self):
        deps = []
        for e in ENGS:
            if self.ecount[e] > 0:
                deps.append(('e', e, self.ecount[e] - 1))
        for b in self.dbufs:
            deps.append(('d', b, b.dcount))
        for e in ENGS:
            self.ops[e].append([list(deps), None, None])

    def finalize(self):
        signal = {e: set() for e in ENGS}
        for e in ENGS:
            view = {}
            for op in self.ops[e]:
                waits = []
                for d in op[0]:
                    if d[0] == 'e':
                        _, e2, idx = d
                        if e2 == e and e == "pe":
                            continue
                        if e2 == e and op[1] is None:
                            continue
                        key = ('e', e2)
                        if view.get(key, -1) >= idx:
                            continue
                        view[key] = idx
                        signal[e2].add(idx)
                        waits.append(d)
                    else:
                        _, b, cnt = d
                        key = ('d', b.name)
                        if view.get(key, 0) >= cnt:
                            continue
                        view[key] = cnt
                        waits.append(d)
                op[0] = waits
        self.rank = {}
        self.esems = {}
        for e in ENGS:
            srt = sorted(signal[e])
            self.rank[e] = {idx: r for r, idx in enumerate(srt)}
            nep = (len(srt) + EPOCH - 1) // EPOCH
            self.esems[e] = [self.stack.enter_context(self.nc.semaphore("s_%s%d" % (e, k)))
                             for k in range(max(nep, 1))]


    def check(self):
        pos = {e: 0 for e in ENGS}
        done_e = {e: -1 for e in ENGS}
        dcnt = {}
        total = sum(len(self.ops[e]) for e in ENGS)
        ndone = 0
        while ndone < total:
            progressed = False
            for e in ENGS:
                while pos[e] < len(self.ops[e]):
                    waits, fn, tag = self.ops[e][pos[e]]
                    ok = True
                    for d in waits:
                        if d[0] == 'e':
                            if done_e[d[1]] < d[2]:
                                ok = False
                                break
                        else:
                            if dcnt.get(d[1].name, 0) < d[2]:
                                ok = False
                                break
                    if not ok:
                        break
                    if tag is not None:
                        if tag[0] == 'e':
                            done_e[e] = tag[1]
                        else:
                            nm = [b for b in self.dbufs if b.dsem is tag[1]][0].name
                            dcnt[nm] = dcnt.get(nm, 0) + 16
                    pos[e] += 1
                    ndone += 1
                    progressed = True
            if not progressed:
                msg = []
                for e in ENGS:
                    if pos[e] < len(self.ops[e]):
                        waits = self.ops[e][pos[e]][0]
                        msg.append((e, pos[e], len(self.ops[e]), [(d[0], d[1] if d[0] == 'e' else d[1].name, d[2]) for d in waits], dict(done_e)))
                raise RuntimeError("DEADLOCK: %r" % (msg,))
        return True

    def emit(self, block):
        sched = self

        def run(eng_name):
            def body(e):
                for waits, fn, tag in sched.ops[eng_name]:
                    for d in waits:
                        if d[0] == 'e':
                            r = sched.rank[d[1]][d[2]]
                            e.wait_ge(sched.esems[d[1]][r // EPOCH], r % EPOCH + 1)
                        else:
                            e.wait_ge(d[1].dsem, d[2])
                    if fn is None:
                        continue
                    ins = fn(e)
                    if tag[0] == 'd':
                        ins.then_inc(tag[1], 16)
                    else:
                        r = sched.rank[eng_name].get(tag[1])
                        if r is not None:
                            ins.then_inc(sched.esems[eng_name][r // EPOCH], 1)
            return body

        block.tensor(run("pe"))
        block.scalar(run("act"))
        block.vector(run("dve"))
        block.gpsimd(run("pool"))
        block.sync(run("sp"))


class Arena:
    def __init__(self, t, n):
        self.t = t
        self.n = n
        self.off = 0

    def alloc(self, name, shape, dt=F32):
        p = shape[0]
        n = int(np.prod(shape[1:]))
        if dt == F32:
            ap = self.t[0:p, self.off:self.off + n]
            self.off += n
        else:
            nf = (n + 1) // 2
            ap = self.t[0:p, self.off:self.off + nf].bitcast(BF16)[:, 0:n]
            self.off += nf
        self.off = (self.off + 7) // 8 * 8
        assert self.off <= self.n, ("arena overflow", name, self.off, self.n)
        if len(shape) == 3:
            ap = ap.rearrange("p (a b) -> p a b", a=shape[1])
        elif len(shape) == 4:
            ap = ap.rearrange("p (a b c) -> p a b c", a=shape[1], b=shape[2])
        return Buf(name, ap)


def build(debug=None):
    nc = bass.Bass("TRN2", target_bir_lowering=False)

    def din(name, shape):
        return nc.dram_tensor(name, list(shape), F32, kind="ExternalInput").ap()

    def dscr(name, shape, dt=BF16):
        return nc.dram_tensor(name, list(shape), dt, kind="Internal").ap()

    xT = din("xT", [2048, 4608])
    xtok = din("xtok", [4096, 2048])
    w_in = din("w_in", [2048, 5120])
    w_out = din("w_out", [2048, 2048])
    wq = din("wq", [2048, 2048])
    uT = din("uT", [2048, 16384])
    vtab = din("vtab", [16384, 2048])
    bcol_d = din("bcol", [128, 40])
    brow_in_d = din("brow_in", [1, 2048])
    brow_out_d = din("brow_out", [1, 2048])
    bs_row_d = din("bs_row", [1, 1024])
    vng_d = din("vng", [128, 1024])
    vnb_d = din("vnb", [128, 1024])
    ln1g_d = din("ln1g", [128, 2048])
    ln1b_d = din("ln1b", [128, 2048])
    ln2g_d = din("ln2g", [128, 2048])
    ln2b_d = din("ln2b", [128, 2048])
    wsT_d = din("wsT", [128, 1024])
    skT_d = din("skT", [128, 2048])
    bias_d = din("biasT", [5, 128, 10240])
    ident_d = din("ident", [128, 128])
    outp = nc.dram_tensor("out", [4096, 2048], F32, kind="ExternalOutput").ap()

    w_in_bf = dscr("w_in_bf", [2048, 5120])
    w_out_bf = dscr("w_out_bf", [2048, 2048])
    wq_bf = dscr("wq_bf", [2048, 2048])
    uT_bf = dscr("uT_bf", [2048, 16384])
    v_bf = dscr("v_bf", [16384, 2048])
    xT_bf = dscr("xT_bf", [2048, 4608])
    bias_bf = dscr("bias_bf", [5, 128, 10240])
    wsT_bf = dscr("wsT_bf", [128, 1024])
    skT_bf = dscr("skT_bf", [128, 2048])
    ident_bfd = dscr("ident_bfd", [128, 128])
    qT_s = dscr("qT_s", [1024, 4608])
    kT_s = dscr("kT_s", [1024, 4608])
    vaug_s = dscr("vaug_s", [4608, 1040])
    AT_s = dscr("AT_s", [1024, 4096])
    xn_s = dscr("xn_s", [4096, 2048], F32)
    xnT_s = dscr("xnT_s", [2048, 4096])

    stack = ExitStack()
    with stack:
        ARN = 52400
        at = stack.enter_context(nc.sbuf_tensor("arena", [128, ARN], F32))
        A = Arena(at, ARN)
        S = Sched(nc, stack)

        def ps(name, n, dt=F32):
            return Buf(name, stack.enter_context(nc.psum_tensor(name, [128, n], dt)))

        pA = [ps("pA0", 512), ps("pA1", 512)]
        pY = ps("pY", 2048)
        pS = ps("pS", 512)
        pT = ps("pT", 1024, BF16)

        ident_f = A.alloc("ident_f", [128, 128])
        ident_b = A.alloc("ident_b", [128, 128], BF16)
        ones_f = A.alloc("ones_f", [1, 128])
        bcol = A.alloc("bcol", [128, 40])
        bq8 = A.alloc("bq8", [128, 8])
        brow_in = A.alloc("brow_in", [1, 2048])
        brow_out = A.alloc("brow_out", [1, 2048])
        bs_row = A.alloc("bs_row", [1, 1024])
        small = A.alloc("small", [128, 64])
        persist_mark = A.off

        cst = Buf("cst", None)
        cst2 = Buf("cst2", None)

        def cast_dma(dst, src, trk=None, extra=()):
            S.dma("pool", lambda e, dst=dst, src=src: e.dma_start(out=dst, in_=src), wr=(trk or cst), extra=extra)

        cx = [Buf("cx%d" % i, None) for i in range(9)]
        cw = [Buf("cw%d" % i, None) for i in range(10)]

        def cast_cols(dst, src, c0, trk):
            for r in range(2):
                cast_dma(dst[r * 1024:(r + 1) * 1024, c0:c0 + 512].rearrange("(a p) n -> p a n", p=128),
                         src[r * 1024:(r + 1) * 1024, c0:c0 + 512].rearrange("(a p) n -> p a n", p=128), trk)

        cast_dma(ident_bfd[:, :], ident_d[:, :])
        cast_dma(wsT_bf[:, :], wsT_d[:, :])
        cast_cols(xT_bf, xT, 0, cx[0])
        for cb in range(10):
            cast_cols(w_in_bf, w_in, cb * 512, cw[cb])
        for blk in range(1, 9):
            cast_cols(xT_bf, xT, blk * 512, cx[blk])
        cast_dma(skT_bf[:, :], skT_d[:, :], cst2, extra=[cx[8]])
        for c in range(5):
            cast_dma(bias_bf[c], bias_d[c], cst2)
        for r in range(4):
            cast_dma(w_out_bf[r * 512:(r + 1) * 512, :].rearrange("(a p) n -> p a n", p=128),
                     w_out[r * 512:(r + 1) * 512, :].rearrange("(a p) n -> p a n", p=128), cst2)
            cast_dma(wq_bf[r * 512:(r + 1) * 512, :].rearrange("(a p) n -> p a n", p=128),
                     wq[r * 512:(r + 1) * 512, :].rearrange("(a p) n -> p a n", p=128), cst2)
        for r in range(16):
            cast_dma(uT_bf[r * 128:(r + 1) * 128, :], uT[r * 128:(r + 1) * 128, :], cst2)
        for r in range(16):
            cast_dma(v_bf[r * 1024:(r + 1) * 1024, :].rearrange("(a p) n -> p a n", p=128),
                     vtab[r * 1024:(r + 1) * 1024, :].rearrange("(a p) n -> p a n", p=128), cst2)

        def ld(buf, dst, src, eng="sp", extra=()):
            S.dma(eng, lambda e, dst=dst, src=src: e.dma_start(out=dst, in_=src), wr=buf, extra=extra)

        def st(buf, dst, src, eng="sp"):
            S.dma(eng, lambda e, dst=dst, src=src: e.dma_start(out=dst, in_=src), rd=buf)

        ld(ident_f, ident_f[:, :], ident_d[:, :])
        ld(bcol, bcol[:, :], bcol_d[:, :])
        ld(brow_in, brow_in[:, :], brow_in_d[:, :])
        ld(brow_out, brow_out[:, :], brow_out_d[:, :])
        ld(bs_row, bs_row[:, :], bs_row_d[:, :])
        ld(ident_b, ident_b[:, :], ident_bfd[:, :], extra=[cst])
        S.op("dve", lambda e: e.memset(ones_f[:, :], 1.0), writes=[ones_f])
        S.op("dve", lambda e: e.tensor_scalar_mul(bq8[:, :], bcol[:, 16:24], 0.125),
             reads=[bcol], writes=[bq8])

        def mm(out, lhsT, rhs, start, stop, reads, writes):
            S.op("pe", lambda e, o=out, l=lhsT, r=rhs, s0=start, s1=stop:
                 e.matmul(o, lhsT=l, rhs=r, start=s0, stop=s1), reads=reads, writes=writes)

        def act(out, in_, func, reads, writes, bias=None, scale=None, ww_ok=False):
            kw = {}
            if bias is not None:
                kw["bias"] = bias
            if scale is not None:
                kw["scale"] = scale
            S.op("act", lambda e, o=out, i=in_, f=func, kw=kw: e.activation(o, i, f, **kw),
                 reads=reads, writes=writes, ww_ok=ww_ok)

        def dve(fn, reads, writes, eng="dve", ww_ok=False):
            S.op(eng, fn, reads=reads, writes=writes, ww_ok=ww_ok)

        def layernorm(src, nfree, g_bc, b_bc, dst, scr):
            nch = nfree // 512
            sAP, sB = src
            dAP, dB = dst
            for c in range(nch):
                dve(lambda e, c=c: e.bn_stats(scr[:, c * 6:(c + 1) * 6], sAP[:, c * 512:(c + 1) * 512]),
                    [sB, scr], [scr])
            dve(lambda e: e.bn_aggr(scr[:, 32:34], scr[:, 0:nch * 6]), [scr], [scr])
            dve(lambda e: e.tensor_scalar_add(scr[:, 34:35], scr[:, 33:34], LN_EPS), [scr], [scr])
            act(scr[:, 35:36], scr[:, 34:35], AF.Ln, [scr], [scr])
            act(scr[:, 36:37], scr[:, 35:36], AF.Exp, [scr], [scr], scale=-0.5)
            dve(lambda e: e.tensor_scalar(out=sAP, in0=sAP, scalar1=scr[:, 32:33], scalar2=scr[:, 36:37],
                                          op0=ALU.subtract, op1=ALU.mult), [sB, scr], [sB])
            gAP, gB = g_bc if isinstance(g_bc, tuple) else (g_bc[:, :], g_bc)
            bAP, bB = b_bc if isinstance(b_bc, tuple) else (b_bc[:, :], b_bc)
            gBl = gB if isinstance(gB, list) else [gB]
            bBl = bB if isinstance(bB, list) else [bB]
            dve(lambda e: e.tensor_tensor(out=sAP, in0=sAP, in1=gAP, op=ALU.mult), [sB] + gBl, [sB])
            dve(lambda e: e.tensor_tensor(out=dAP, in0=sAP, in1=bAP, op=ALU.add), [sB] + bBl, [dB])

        W = [A.alloc("W0", [128, 16, 512], BF16), A.alloc("W1", [128, 16, 512], BF16)]
        mark_w = A.off
        XB = [A.alloc("XB0", [128, 16, 512], BF16), A.alloc("XB1", [128, 16, 512], BF16)]
        stage_mark = A.off
        W1s = W + [A.alloc("W2", [128, 16, 512], BF16), A.alloc("W3", [128, 16, 512], BF16)]
        uTb = A.alloc("uTb", [128, 8, 512])
        vg = A.alloc("vg", [128, 4, 1024])
        vng = A.alloc("vng", [128, 1024])
        vnb = A.alloc("vnb", [128, 1024])
        vn = A.alloc("vn", [128, 1024], BF16)
        vaug = [A.alloc("vaug%d" % i, [128, 16, 65], BF16) for i in range(4)]
        qk = [A.alloc("qk0", [128, 512], BF16), A.alloc("qk1", [128, 512], BF16)]
        ATb = A.alloc("ATb", [128, 8, 128], BF16)
        wsT = A.alloc("wsT", [128, 8, 128], BF16)
        ld(vng, vng[:, :], vng_d[:, :])
        ld(vnb, vnb[:, :], vnb_d[:, :])
        ld(wsT, wsT[:, :, :], wsT_bf.rearrange("p (g q) -> p g q", g=8), extra=[cst])
        for i in range(4):
            dve(lambda e, i=i: e.memset(vaug[i][:, :, 64:65], 1.0), [], [vaug[i]])

        w_in_v = w_in_bf.rearrange("(kc p) n -> p kc n", p=128)
        xT_v = xT_bf.rearrange("(kc p) n -> p kc n", p=128)
        qT_v = qT_s.rearrange("(j p) n -> p j n", p=128)
        kT_v = kT_s.rearrange("(j p) n -> p j n", p=128)
        AT_v = AT_s.rearrange("(g p) n -> p g n", p=128)
        wcount = 0
        pacount = 0
        qkcount = 0
        NBLK1 = 9
        for blk in range(NBLK1):
            xb = XB[blk % 2]
            for kq in range(4):
                ld(xb, xb[:, kq * 4:(kq + 1) * 4, :], xT_v[:, kq * 4:(kq + 1) * 4, blk * 512:(blk + 1) * 512],
                   extra=[cx[blk]])
            for cb in range(10):
                wb = W1s[wcount % 4]
                wcount += 1
                for kq in range(4):
                    ld(wb, wb[:, kq * 4:(kq + 1) * 4, :], w_in_v[:, kq * 4:(kq + 1) * 4, cb * 512:(cb + 1) * 512],
                       extra=[cw[cb]])
                if cb in (0, 1, 4, 5, 6, 7):
                    for g in range(4):
                        pa = pA[pacount % 2]
                        pacount += 1
                        for kc in range(16):
                            mm(pa[:, :], wb[:, kc, g * 128:(g + 1) * 128], xb[:, kc, :], kc == 0, kc == 15,
                               [wb, xb], [pa])
                        if cb < 2:
                            gi = cb * 4 + g
                            act(uTb[:, gi, :], pa[:, :], AF.Gelu, [pa, bcol], [uTb], bias=bcol[:, gi:gi + 1])
                        else:
                            j = (cb - 4) * 4 + g if cb < 6 else (cb - 6) * 4 + g
                            ob = qk[qkcount % 2]
                            qkcount += 1
                            if cb < 6:
                                act(ob[:, :], pa[:, :], AF.Identity, [pa, bq8], [ob], bias=bq8[:, j:j + 1], scale=0.125)
                                st(ob, qT_v[:, j, blk * 512:(blk + 1) * 512], ob[:, :])
                            else:
                                act(ob[:, :], pa[:, :], AF.Identity, [pa, bcol], [ob], bias=bcol[:, 24 + j:25 + j])
                                st(ob, kT_v[:, j, blk * 512:(blk + 1) * 512], ob[:, :])
                else:
                    for tt in range(4):
                        pa = pA[pacount % 2]
                        pacount += 1
                        boff = (cb - 2) * 512 if cb < 4 else 1024 + (cb - 8) * 512
                        mm(pa[:, :], ones_f[0:1, :], brow_in[0:1, boff:boff + 512], True, False,
                           [ones_f, brow_in], [pa])
                        for kc in range(16):
                            mm(pa[:, :], xb[:, kc, tt * 128:(tt + 1) * 128], wb[:, kc, :], False, kc == 15,
                               [wb, xb], [pa])
                        if cb < 4:
                            act(vg[:, tt, (cb - 2) * 512:(cb - 1) * 512], pa[:, :], AF.Gelu, [pa], [vg])
                        else:
                            h0 = (cb - 8) * 8
                            act(vaug[tt][:, h0:h0 + 8, 0:64], pa[:, :].rearrange("p (h d) -> p h d", h=8),
                                AF.Copy, [pa], [vaug[tt]])
            for tt in range(4):
                et = blk * 4 + tt
                st(vaug[tt], vaug_s[et * 128:(et + 1) * 128, :], vaug[tt][:, :, :].rearrange("p h d -> p (h d)"))
                if et < 2 or et >= 34:
                    continue
                t = et - 2
                vgt = vg[:, tt, :]
                layernorm((vgt, vg), 1024, vng, vnb, (vn[:, :], vn), small)
                for g in range(8):
                    mm(pY[:, g * 128:(g + 1) * 128], ones_f[0:1, :], bs_row[0:1, g * 128:(g + 1) * 128], True, False,
                       [ones_f, bs_row], [pY])
                    mm(pY[:, g * 128:(g + 1) * 128], vn[:, g * 128:(g + 1) * 128], wsT[:, g, :], False, True,
                       [vn, wsT], [pY])
                dve(lambda e, tt=tt: e.tensor_tensor(out=ATb[:, :, :],
                                                     in0=pY[:, 0:1024].rearrange("p (g q) -> p g q", g=8),
                                                     in1=uTb[:, :, tt * 128:(tt + 1) * 128], op=ALU.mult),
                    [pY, uTb], [ATb])
                st(ATb, AT_v[:, :, t * 128:(t + 1) * 128], ATb[:, :, :])

        S.barrier()
        A.off = persist_mark
        WO = A.alloc("WO", [128, 16, 2048], BF16)
        biasT = A.alloc("biasT", [128, 16, 5, 128], BF16)
        kTw = A.alloc("kTw", [128, 8, 640], BF16)
        vaw = A.alloc("vaw", [128, 5, 1040], BF16)
        qTt = A.alloc("qTt", [128, 8, 128], BF16)
        PT = A.alloc("PT", [128, 640], BF16)
        PT2 = A.alloc("PT2", [128, 640], BF16)
        Bb = A.alloc("Bb", [128, 1024], BF16)
        BT = A.alloc("BT", [128, 8, 128], BF16)
        ATt = A.alloc("ATt", [128, 8, 128], BF16)
        xnb = A.alloc("xnb", [128, 2048], BF16)
        xnT = A.alloc("xnT", [128, 16, 128], BF16)
        xt = A.alloc("xt", [128, 2048])
        rr = A.alloc("rr", [128, 2048])
        ln1g = A.alloc("ln1g", [128, 2048])
        ln1b = A.alloc("ln1b", [128, 2048])
        rec = A.alloc("rec", [128, 8])
        ld(ln1g, ln1g[:, :], ln1g_d[:, :])
        ld(ln1b, ln1b[:, :], ln1b_d[:, :])
        wo_v = w_out_bf.rearrange("(kc p) n -> p kc n", p=128)
        for kq in range(8):
            ld(WO, WO[:, kq * 2:(kq + 1) * 2, :], wo_v[:, kq * 2:(kq + 1) * 2, :])
        vaug_v = vaug_s.rearrange("(e p) f -> p e f", p=128)
        xnT_v = xnT_s.rearrange("(kc p) n -> p kc n", p=128)
        loaded_cls = -1
        NT = 32
        for t in range(NT):
            cls = 0 if t == 0 else 1 if t == 1 else 3 if t == 30 else 4 if t == 31 else 2
            if cls != loaded_cls:
                bv = bias_bf[cls].rearrange("p (h c q) -> p h c q", h=16, c=5)
                for hq in range(4):
                    ld(biasT, biasT[:, hq * 4:(hq + 1) * 4, :, :], bv[:, hq * 4:(hq + 1) * 4, :, :])
                loaded_cls = cls
            ld(qTt, qTt[:, :, :], qT_v[:, :, (t + 2) * 128:(t + 3) * 128])
            ld(kTw, kTw[:, :, :], kT_v[:, :, t * 128:t * 128 + 640])
            ld(vaw, vaw[:, :, :], vaug_v[:, t:t + 5, :])
            ld(ATt, ATt[:, :, :], AT_v[:, :, t * 128:(t + 1) * 128])
            ld(xt, xt[:, :], xtok[t * 128:(t + 1) * 128, :])
            def na_S(h):
                j, base = h // 2, 64 * (h % 2)
                for ck in range(5):
                    if h % 2 == 0:
                        dst, dB = (pS[:, ck * 128:(ck + 1) * 128], pS) if ck < 4 else (pA[0][:, 0:128], pA[0])
                    else:
                        dst, dB = pY[:, ck * 128:(ck + 1) * 128], pY
                    mm(dst, kTw[base:base + 64, j, ck * 128:(ck + 1) * 128], qTt[base:base + 64, j, :], True, False,
                       [kTw, qTt], [dB])
                    mm(dst, ident_b[:, :], biasT[:, h, ck, :], False, True, [ident_b, biasT], [dB])

            def na_exp(h):
                if h % 2 == 0:
                    act(PT[:, 0:512], pS[:, :], AF.Exp, [pS], [PT])
                    act(PT[:, 512:640], pA[0][:, 0:128], AF.Exp, [pA[0]], [PT])
                else:
                    act(PT2[:, 0:640], pY[:, 0:640], AF.Exp, [pY], [PT2])

            def na_PV(h):
                pt_ = PT if h % 2 == 0 else PT2
                ob, oB = (pA[1][:, 0:65], pA[1]) if h % 2 == 0 else (pY[:, 1024:1089], pY)
                for ck in range(5):
                    mm(ob, pt_[:, ck * 128:(ck + 1) * 128], vaw[:, ck, h * 65:(h + 1) * 65], ck == 0, ck == 4,
                       [pt_, vaw], [oB])
                rc = rec[:, (h % 2):(h % 2) + 1]
                dve(lambda e, ob=ob, rc=rc: e.reciprocal(rc, ob[:, 64:65]), [oB], [rec])
                dve(lambda e, h=h, ob=ob, rc=rc: e.tensor_scalar_mul(Bb[:, h * 64:(h + 1) * 64], ob[:, 0:64], rc),
                    [oB, rec], [Bb], ww_ok=True)

            na_S(0)
            na_exp(0)
            for h in range(16):
                if h + 1 < 16:
                    na_S(h + 1)
                    na_exp(h + 1)
                na_PV(h)
            for j in range(8):
                S.op("pe", lambda e, j=j: e.transpose(pT[:, j * 128:(j + 1) * 128], Bb[:, j * 128:(j + 1) * 128],
                                                      ident_b[:, :]), reads=[Bb, ident_b], writes=[pT])
            dve(lambda e: e.tensor_copy(BT[:, :, :], pT[:, :].rearrange("p (j q) -> p j q", j=8)), [pT], [BT])
            for nb in range(4):
                dst = pY[:, nb * 512:(nb + 1) * 512]
                mm(dst, ones_f[0:1, :], brow_out[0:1, nb * 512:(nb + 1) * 512], True, False, [ones_f, brow_out], [pY])
                for g in range(8):
                    mm(dst, ATt[:, g, :], WO[:, g, nb * 512:(nb + 1) * 512], False, False, [ATt, WO], [pY])
                for j in range(8):
                    mm(dst, BT[:, j, :], WO[:, 8 + j, nb * 512:(nb + 1) * 512], False, j == 7, [BT, WO], [pY])
            dve(lambda e: e.scalar_tensor_tensor(out=rr[:, :], in0=xt[:, :], scalar=ALPHA, in1=pY[:, :],
                                                 op0=ALU.mult, op1=ALU.add), [xt, pY], [rr])
            layernorm((rr[:, :], rr), 2048, ln1g, ln1b, (rr[:, :], rr), small)
            st(rr, xn_s[t * 128:(t + 1) * 128, :], rr[:, :])
            act(xnb[:, :], rr[:, :], AF.Copy, [rr], [xnb])
            for half in range(2):
                for j in range(8):
                    kc = half * 8 + j
                    S.op("pe", lambda e, j=j, kc=kc: e.transpose(pT[:, j * 128:(j + 1) * 128],
                                                                 xnb[:, kc * 128:(kc + 1) * 128], ident_b[:, :]),
                         reads=[xnb, ident_b], writes=[pT])
                dve(lambda e, half=half: e.tensor_copy(xnT[:, half * 8:(half + 1) * 8, :],
                                                       pT[:, :].rearrange("p (j q) -> p j q", j=8)), [pT], [xnT])
            st(xnT, xnT_v[:, :, t * 128:(t + 1) * 128], xnT[:, :, :])

        S.barrier()
        A.off = mark_w
        TB = 256
        NTT = TB // 128
        XB3 = [A.alloc("XB3_0", [128, 16, TB], BF16), A.alloc("XB3_1", [128, 16, TB], BF16)]
        V = [A.alloc("V0", [128, 4, 2048], BF16), A.alloc("V1", [128, 4, 2048], BF16)]
        qpT = V[0]
        qpT_ap = V[0][:, :, :].rearrange("p a b -> p (a b)").rearrange("p (c t) -> p c t", c=16)
        skT = A.alloc("skT", [128, 16, 128], BF16)
        mg_off = A.off
        MG = [A.alloc("MGa", [128, 4, 512], BF16), A.alloc("MGb", [128, 4, 512], BF16)]
        mk_off = A.off
        MK = [A.alloc("MKa", [128, 4, 512], BF16), A.alloc("MKb", [128, 4, 512], BF16)]
        assert mk_off - mg_off == 2048 and A.off - mk_off == 2048
        ln2g_ap = at[:, mg_off:mg_off + 2048]
        ln2b_ap = at[:, mk_off:mk_off + 2048]
        PB = [A.alloc("PBa", [128, 4, 512], BF16)]
        PG1 = A.alloc("PG1", [128, 4, 4, 128])
        r_sb = A.alloc("r_sb", [128, NTT, 16, 128])
        Wt = A.alloc("Wt", [128, 512], BF16)
        WTb = [A.alloc("WTb0", [128, 512], BF16), A.alloc("WTb1", [128, 512], BF16)]
        gH = [A.alloc("gH0", [128, 512]), A.alloc("gH1", [128, 512])]
        s_sb = A.alloc("s_sb", [128, NTT, 16, 128])
        yacc = A.alloc("yacc", [128, NTT, 2048])
        xres = A.alloc("xres", [128, 2048])
        top = A.alloc("top", [128, 16, 16])
        cand = A.alloc("cand", [128, 4, 256])
        c16 = A.alloc("c16", [128, 8, 16])
        e16 = A.alloc("e16", [128, 8, 16])
        tmp = A.alloc("tmp", [128, 256])
        sm3 = A.alloc("sm3", [128, 64])
        biasg = A.alloc("biasg", [128, NTT, 8])
        cth = A.alloc("cth", [128, NTT, 8])
        ld(skT, skT[:, :, :], skT_bf.rearrange("p (c n) -> p c n", c=16))
        print("stage3 arena", A.off, A.n)
        wq_v = wq_bf.rearrange("(kc p) n -> p kc n", p=128)
        uT_v = uT_bf.rearrange("(kc p) n -> p kc n", p=128)
        NEB = 32
        NIT = NEB * NTT
        for tb in range(4096 // TB):
            xb = XB3[tb % 2]
            for kq in range(4):
                ld(xb, xb[:, kq * 4:(kq + 1) * 4, :], xnT_v[:, kq * 4:(kq + 1) * 4, tb * TB:(tb + 1) * TB])
            for cbq in range(4):
                wb = W[wcount % 2]
                wcount += 1
                for kq in range(4):
                    ld(wb, wb[:, kq * 4:(kq + 1) * 4, :], wq_v[:, kq * 4:(kq + 1) * 4, cbq * 512:(cbq + 1) * 512])
                for g in range(4):
                    c = cbq * 4 + g
                    pa = pA[pacount % 2]
                    pacount += 1
                    for kc in range(16):
                        mm(pa[:, 0:TB], wb[:, kc, g * 128:(g + 1) * 128], xb[:, kc, :], kc == 0, kc == 15,
                           [wb, xb], [pa])
                    act(qpT_ap[:, c, 0:TB], pa[:, 0:TB], AF.Copy, [pa], [qpT])
            for tt in range(NTT):
                for cg in range(4):
                    pa = pA[pacount % 2]
                    pacount += 1
                    for ci in range(4):
                        c = cg * 4 + ci
                        mm(pa[:, ci * 128:(ci + 1) * 128], qpT_ap[:, c, tt * 128:(tt + 1) * 128], skT[:, c, :],
                           True, True, [qpT, skT], [pa])
                    dve(lambda e, tt=tt, cg=cg, pa=pa: e.tensor_copy(
                        s_sb[:, tt, cg * 4:(cg + 1) * 4, :], pa[:, :].rearrange("p (c n) -> p c n", c=4)),
                        [pa], [s_sb])
                for c in range(16):
                    dve(lambda e, tt=tt, c=c: e.max(out=top[:, c, 0:8], in_=s_sb[:, tt, c, :]), [s_sb], [top])
                    dve(lambda e, tt=tt, c=c: e.match_replace(out=tmp[:, 0:128], in_to_replace=top[:, c, 0:8],
                                                              in_values=s_sb[:, tt, c, :], imm_value=-1e30),
                        [s_sb, top], [tmp])
                    dve(lambda e, c=c: e.max(out=top[:, c, 8:16], in_=tmp[:, 0:128]), [tmp], [top])
                topv = top[:, :, :].rearrange("p (h two) k -> p h two k", two=2)
                for hh in range(2):
                    hs = slice(hh * 4, hh * 4 + 4)
                    dve(lambda e, hs=hs: e.tensor_tensor(
                        out=cand[:, :, :].rearrange("p h (a b) -> p h a b", a=16),
                        in0=topv[:, hs, 0, :].unsqueeze(3).to_broadcast([128, 4, 16, 16]),
                        in1=topv[:, hs, 1, :].unsqueeze(2).to_broadcast([128, 4, 16, 16]), op=ALU.add),
                        [top], [cand])
                    for h4 in range(4):
                        h = hh * 4 + h4
                        dve(lambda e, h=h, h4=h4: e.max(out=c16[:, h, 0:8], in_=cand[:, h4, :]), [cand], [c16])
                        dve(lambda e, h=h, h4=h4: e.match_replace(out=tmp[:, :], in_to_replace=c16[:, h, 0:8],
                                                                  in_values=cand[:, h4, :], imm_value=-1e30),
                            [cand, c16], [tmp])
                        dve(lambda e, h=h: e.max(out=c16[:, h, 8:16], in_=tmp[:, :]), [tmp], [c16])
                dve(lambda e: e.tensor_scalar_add(sm3[:, 0:8], c16[:, :, 15], -TAU_EPS), [c16], [sm3])
                dve(lambda e: e.tensor_tensor(out=e16[:, :, :], in0=c16[:, :, :],
                                              in1=c16[:, :, 0:1].to_broadcast([128, 8, 16]), op=ALU.subtract),
                    [c16], [e16])
                act(e16[:, :, :], e16[:, :, :], AF.Exp, [e16], [e16])
                dve(lambda e: e.reduce_sum(out=sm3[:, 8:16], in_=e16[:, :, :], axis=AX.X), [e16, sm3], [sm3])
                act(sm3[:, 16:24], sm3[:, 8:16], AF.Ln, [sm3], [sm3])
                dve(lambda e: e.tensor_tensor(out=sm3[:, 24:32], in0=sm3[:, 0:8], in1=c16[:, :, 0], op=ALU.subtract),
                    [sm3, c16], [sm3])
                dve(lambda e, tt=tt: e.tensor_tensor(out=biasg[:, tt, :], in0=sm3[:, 24:32], in1=sm3[:, 16:24],
                                                     op=ALU.subtract), [sm3], [biasg])
                act(cth[:, tt, :], biasg[:, tt, :], AF.Exp, [biasg], [cth])
                dve(lambda e, tt=tt: e.tensor_scalar_mul(cth[:, tt, :], cth[:, tt, :], -1.0), [cth], [cth])
                dve(lambda e, tt=tt, topv=topv: e.tensor_tensor(out=sm3[:, 32:40], in0=biasg[:, tt, :],
                                                                in1=topv[:, :, 1, 0], op=ALU.add),
                    [biasg, top, sm3], [sm3])
                dve(lambda e: e.tensor_tensor(out=sm3[:, 32:40], in0=sm3[:, 32:40], in1=sm3[:, 0:8],
                                              op=ALU.subtract), [sm3], [sm3])
                sv = s_sb[:, tt, :, :].rearrange("p (h two) n -> p h two n", two=2)
                rv = r_sb[:, tt, :, :].rearrange("p (h two) n -> p h two n", two=2)
                dve(lambda e, sv=sv, rv=rv: e.tensor_tensor(
                    out=rv[:, :, 0, :], in0=sm3[:, 0:8].unsqueeze(2).to_broadcast([128, 8, 128]),
                    in1=sv[:, :, 0, :], op=ALU.subtract), [s_sb, sm3], [r_sb])
                dve(lambda e, sv=sv, rv=rv: e.tensor_copy(rv[:, :, 1, :], sv[:, :, 1, :]), [s_sb], [r_sb])
                dve(lambda e, sv=sv: e.tensor_tensor(out=sv[:, :, 0, :], in0=sv[:, :, 0, :],
                                                     in1=sm3[:, 32:40].unsqueeze(2).to_broadcast([128, 8, 128]),
                                                     op=ALU.add), [s_sb, sm3], [s_sb])
                dve(lambda e, sv=sv, topv=topv: e.tensor_tensor(
                    out=sv[:, :, 1, :], in0=sv[:, :, 1, :],
                    in1=topv[:, :, 1, 0:1].to_broadcast([128, 8, 128]), op=ALU.subtract),
                    [s_sb, top], [s_sb])
                act(s_sb[:, tt, :, :], s_sb[:, tt, :, :], AF.Exp, [s_sb], [s_sb])

            wbase = wcount
            wcount += NEB

            def load_W(eb):
                wb = W[(wbase + eb) % 2]
                for kq in range(4):
                    ld(wb, wb[:, kq * 4:(kq + 1) * 4, :], uT_v[:, kq * 4:(kq + 1) * 4, eb * 512:(eb + 1) * 512])

            def load_V(eb):
                vb = V[eb % 2]
                vsrc = v_bf[eb * 512:(eb + 1) * 512, :].rearrange("(c p) n -> p c n", p=128)
                for c in range(4):
                    ld(vb, vb[:, c, :], vsrc[:, c, :])

            def st_P(it):
                eb, tt = divmod(it, NTT)
                i0 = eb * 4
                rv = r_sb[:, tt, :, :].rearrange("p (h two) n -> p h two n", two=2)
                sv = s_sb[:, tt, :, :].rearrange("p (h two) n -> p h two n", two=2)
                dve(lambda e, rv=rv, i0=i0: e.tensor_tensor(
                    out=MK[0][:, :, :].rearrange("p h (a j) -> p h a j", a=4),
                    in0=rv[:, 0:4, 1, :].unsqueeze(2).to_broadcast([128, 4, 4, 128]),
                    in1=rv[:, 0:4, 0, i0:i0 + 4].unsqueeze(3).to_broadcast([128, 4, 4, 128]),
                    op=ALU.subtract), [r_sb], [MK[0]], eng="pool")
                dve(lambda e, sv=sv, i0=i0: e.tensor_tensor(
                    out=PG1[:, :, :, :],
                    in0=sv[:, 4:8, 0, i0:i0 + 4].unsqueeze(3).to_broadcast([128, 4, 4, 128]),
                    in1=sv[:, 4:8, 1, :].unsqueeze(2).to_broadcast([128, 4, 4, 128]),
                    op=ALU.mult), [s_sb], [PG1], eng="pool")

            def st_PB(it):
                eb, tt = divmod(it, NTT)
                i0 = eb * 4
                sv = s_sb[:, tt, :, :].rearrange("p (h two) n -> p h two n", two=2)
                for hh in range(4):
                    for a in range(4):
                        act(PB[0][:, hh, a * 128:(a + 1) * 128], sv[:, hh, 1, :], AF.Copy, [s_sb], [PB[0]],
                            scale=sv[:, hh, 0, i0 + a:i0 + a + 1], ww_ok=True)
                for hh in range(4):
                    h = 4 + hh
                    pv = PG1[:, hh, :, :].rearrange("p a j -> p (a j)")
                    act(MK[1][:, hh, :], pv, AF.Sign, [PG1, cth], [MK[1]], bias=cth[:, tt, h:h + 1], ww_ok=True)

            def st_M(it):
                for hh in range(4):
                    dve(lambda e, hh=hh: e.scalar_tensor_tensor(
                        out=MG[0][:, hh, :], in0=MK[0][:, hh, :], scalar=0.0, in1=PB[0][:, hh, :],
                        op0=ALU.is_ge, op1=ALU.mult), [MK[0], PB[0]], [MG[0]], ww_ok=True)
                for hh in range(4):
                    pv = PG1[:, hh, :, :].rearrange("p a j -> p (a j)")
                    dve(lambda e, hh=hh, pv=pv: e.scalar_tensor_tensor(
                        out=MG[1][:, hh, :], in0=MK[1][:, hh, :], scalar=0.0, in1=pv,
                        op0=ALU.max, op1=ALU.mult), [MK[1], PG1], [MG[1]], ww_ok=True)

            def st_Gs(it):
                for h in range(8):
                    mm(pS[:, :], ident_b[:, :], MG[h // 4][:, h % 4, :], h == 0, h == 7, [ident_b, MG[h // 4]], [pS])

            def st_H(it):
                eb, tt = divmod(it, NTT)
                wb = W[(wbase + eb) % 2]
                pa = pA[it % 2]
                for kc in range(16):
                    mm(pa[:, :], xb[:, kc, tt * 128:(tt + 1) * 128], wb[:, kc, :], kc == 0, kc == 15, [wb, xb], [pa])

            def st_gelu(it):
                act(gH[it % 2][:, :], pA[it % 2][:, :], AF.Gelu, [pA[it % 2]], [gH[it % 2]])

            def st_W(it):
                g_ = gH[it % 2]
                dve(lambda e, g_=g_: e.tensor_tensor(out=Wt[:, :], in0=g_[:, :], in1=pS[:, :], op=ALU.mult),
                    [g_, pS], [Wt])

            def st_T(it):
                for c in range(4):
                    S.op("pe", lambda e, c=c: e.transpose(pT[:, c * 128:(c + 1) * 128],
                                                          Wt[:, c * 128:(c + 1) * 128], ident_b[:, :]),
                         reads=[Wt, ident_b], writes=[pT])

            def st_WTc(it):
                act(WTb[it % 2][:, :], pT[:, 0:512], AF.Copy, [pT], [WTb[it % 2]])

            def st_y(it):
                eb, tt = divmod(it, NTT)
                vb = V[eb % 2]
                wt_ = WTb[it % 2]
                for nb in range(4):
                    for c in range(4):
                        mm(pY[:, nb * 512:(nb + 1) * 512], wt_[:, c * 128:(c + 1) * 128],
                           vb[:, c, nb * 512:(nb + 1) * 512], c == 0, c == 3, [wt_, vb], [pY])

            def st_yacc(it):
                eb, tt = divmod(it, NTT)
                if eb == 0:
                    dve(lambda e, tt=tt: e.tensor_copy(yacc[:, tt, :], pY[:, :]), [pY], [yacc])
                else:
                    dve(lambda e, tt=tt: e.tensor_tensor(out=yacc[:, tt, :], in0=yacc[:, tt, :], in1=pY[:, :],
                                                         op=ALU.add), [pY, yacc], [yacc])

            load_W(0)
            load_W(1)
            load_V(0)
            load_V(1)
            st_P(0)
            st_PB(0)
            st_M(0)
            st_H(0)
            st_gelu(0)
            for r in range(NIT + 2):
                if r < NIT:
                    st_Gs(r)
                if r + 1 < NIT:
                    if (r + 1) % NTT == 0:
                        ebn = (r + 1) // NTT + 1
                        if ebn < NEB:
                            load_W(ebn)
                    st_P(r + 1)
                    st_PB(r + 1)
                    st_H(r + 1)
                if r >= 2:
                    st_yacc(r - 2)
                if r < NIT:
                    st_W(r)
                    st_T(r)
                if r + 1 < NIT:
                    st_M(r + 1)
                    st_gelu(r + 1)
                if r < NIT:
                    st_WTc(r)
                if 1 <= r <= NIT:
                    st_y(r - 1)
                if r >= NTT and r % NTT == 0 and r // NTT + 1 < NEB:
                    load_V(r // NTT + 1)
            ld(MG[0], ln2g_ap[:, 0:1024], ln2g_d[:, 0:1024])
            ld(MG[1], ln2g_ap[:, 1024:2048], ln2g_d[:, 1024:2048])
            ld(MK[0], ln2b_ap[:, 0:1024], ln2b_d[:, 0:1024])
            ld(MK[1], ln2b_ap[:, 1024:2048], ln2b_d[:, 1024:2048])
            for tt in range(NTT):
                t = tb * NTT + tt
                ld(xres, xres[:, :], xn_s[t * 128:(t + 1) * 128, :])
                dve(lambda e, tt=tt: e.scalar_tensor_tensor(out=xres[:, :], in0=xres[:, :], scalar=ALPHA,
                                                            in1=yacc[:, tt, :], op0=ALU.mult, op1=ALU.add),
                    [xres, yacc], [xres])
                layernorm((xres[:, :], xres), 2048, (ln2g_ap, [MG[0], MG[1]]), (ln2b_ap, [MK[0], MK[1]]), (xres[:, :], xres), sm3)
                st(xres, outp[t * 128:(t + 1) * 128, :], xres[:, :])
            S.barrier()

        S.barrier()
        S.finalize()
        S.check()
        with nc.Block() as block:
            S.emit(block)
    return nc


def _bias_tables(rpb, q):
    out = np.full((5, 128, 16, 5, 128), NEG, np.float32)
    qc = np.arange(64)
    cs = np.clip(qc - 8, 0, 48)
    kc = np.arange(64)
    colok = (kc[:, None] >= cs[None, :]) & (kc[:, None] < cs[None, :] + 16)
    coff = kc[:, None] - qc[None, :] + 15
    for ci, t in enumerate((0, 1, 2, 30, 31)):
        for qp in range(2):
            R = 64 * q + 2 * t + qp
            rs = min(max(R - 4, 0), 248)
            seen = set()
            order = list(range(5))
            slots = []
            for ck in order:
                for kp in range(2):
                    lr = 2 * t - 4 + 2 * ck + kp
                    wrapped = (q == 0 and lr < 0) or (q == 3 and lr >= 64)
                    KR = 64 * q + lr + (8 if (q == 0 and lr < 0) else 0) - (8 if (q == 3 and lr >= 64) else 0)
                    slots.append((wrapped, ck, kp, KR))
            slots.sort(key=lambda s: s[0])
            for wrapped, ck, kp, KR in slots:
                if KR < rs or KR >= rs + 8 or KR in seen:
                    continue
                seen.add(KR)
                ro = KR - R + 7
                vals = rpb[:, ro, :][:, np.clip(coff, 0, 30)]
                vals = np.where(colok[None], vals, NEG)
                out[ci, kp * 64:(kp + 1) * 64, :, ck, qp * 64:(qp + 1) * 64] = vals.transpose(1, 0, 2)
            assert len(seen) == 8
    return out.reshape(5, 128, 16 * 5 * 128)


_NC_CACHE = {}


def kernel(x, w_in, b_in, v_norm_g, v_norm_b, w_spatial, b_spatial, rpb, w_out, b_out,
           ln1_g, ln1_b, peer_wq, peer_subkeys, peer_u, peer_v, ln2_g, ln2_b):
    f = lambda a: np.ascontiguousarray(np.asarray(a, dtype=np.float32))
    x = f(x)
    bc = lambda v, n: f(np.broadcast_to(np.asarray(v, np.float32).reshape(1, n), (128, n)))
    b_in0 = np.asarray(b_in, np.float32)[0]
    shared = {
        "w_in": f(w_in[0]), "w_out": f(w_out[0]), "wq": f(peer_wq[0]),
        "uT": f(np.asarray(peer_u[0]).T), "vtab": f(peer_v[0]),
        "bcol": f(b_in0.reshape(40, 128).T),
        "brow_in": f(np.concatenate([b_in0[1024:2048], b_in0[4096:5120]]).reshape(1, 2048)),
        "brow_out": f(np.asarray(b_out[0]).reshape(1, 2048)),
        "bs_row": f(np.asarray(b_spatial[0]).reshape(1, 1024)),
        "vng": bc(v_norm_g[0], 1024), "vnb": bc(v_norm_b[0], 1024),
        "ln1g": bc(ln1_g[0], 2048), "ln1b": bc(ln1_b[0], 2048),
        "ln2g": bc(ln2_g[0], 2048), "ln2b": bc(ln2_b[0], 2048),
        "wsT": f(np.asarray(w_spatial[0]).transpose(2, 0, 1).reshape(128, 1024)),
        "skT": f(np.asarray(peer_subkeys[0]).transpose(3, 0, 1, 2).reshape(128, 2048)),
        "ident": np.eye(128, dtype=np.float32),
    }
    rpb0 = np.asarray(rpb[0], np.float32)
    tabs = [_bias_tables(rpb0, q) for q in range(4)]
    in_maps = []
    for c in range(8):
        b, q = c // 4, c % 4
        xb = x[b]
        rows = []
        for lr in range(-4, 68):
            gr = 64 * q + lr
            if q == 0 and lr < 0:
                gr = lr + 8
            if q == 3 and lr >= 64:
                gr = 64 * q + lr - 8
            rows.append(gr)
        rows = np.asarray(rows)
        tok = (rows[:, None] * 64 + np.arange(64)[None, :]).reshape(-1)
        xext = xb[tok]
        m = dict(shared)
        m["xT"] = np.ascontiguousarray(xext.T)
        m["xtok"] = np.ascontiguousarray(xb[4096 * q:4096 * (q + 1)])
        m["biasT"] = tabs[q]
        in_maps.append(m)
    if "nc" not in _NC_CACHE:
        _NC_CACHE["nc"] = build()
    res = run_bass_kernel_spmd(_NC_CACHE["nc"], in_maps, core_ids=list(range(8)))
    out = np.empty((2, 16384, 2048), np.float32)
    for c in range(8):
        b, q = c // 4, c % 4
        out[b, 4096 * q:4096 * (q + 1)] = res.results[c]["out"]
    return out
```
